# Optimizing a Trainium2 kernel written in Bass

```python
import math
import jax, jax.numpy as jnp
from jax import lax
import numpy as np

D_MODEL = 1024
BATCH = 4
SEQ = 4096
DEPTH = 1
DEC_BATCH = 8
DEC_SEQ = 8192
PAST_LEN = 128

HEAD_DIM = 64
H_A = 8
N_GROUPS_B = 3
HG_B = 4
H_B = N_GROUPS_B * HG_B
WINDOWS_B = (128, 512, 2048)
DILATIONS_B = (1, 4, 16)
BLK = 64
GRID_W = 64
WIN_ROWS = 8
WIN_COLS = 16
NUM_BUCKETS = 32
T5_MAX_DIST = 1024
D_FF = 4 * D_MODEL
EPS = 1e-6
NEG = -1e30
QA_W = H_A * HEAD_DIM
QB_W = H_B * HEAD_DIM
IN_W = 3 * QA_W + 3 * QB_W + 2 * D_MODEL

kernel_name = "hybrid_natten_dilated_encoder"


def rmsnorm(x, g):
    xf = x.astype(jnp.float32)
    y = xf * lax.rsqrt(jnp.mean(xf * xf, axis=-1, keepdims=True) + EPS)
    return (y * g.astype(jnp.float32)).astype(x.dtype)


def t5_buckets(rel):
    half = NUM_BUCKETS // 2
    ret = np.where(rel > 0, half, 0)
    n = np.abs(rel)
    max_exact = half // 2
    large = max_exact + (np.log(np.maximum(n, 1) / max_exact)
                         / np.log(T5_MAX_DIST / max_exact) * (half - max_exact)).astype(np.int32)
    large = np.minimum(large, half - 1)
    return (ret + np.where(n < max_exact, n, large)).astype(np.int32)


def neighborhood_attention(q, k, v, rpb):
    B, H, T, hd = q.shape
    rows = T // GRID_W
    kr = min(WIN_ROWS, rows)
    qg = q.reshape(B, H, rows, GRID_W, hd)
    kg = k.reshape(B, H, rows, GRID_W, hd)
    vg = v.reshape(B, H, rows, GRID_W, hd)
    col = np.arange(GRID_W)
    col_start = np.clip(col - WIN_COLS // 2, 0, GRID_W - WIN_COLS)
    col_idx = col_start[:, None] + np.arange(WIN_COLS)[None, :]
    col_off = col_idx - col[:, None]
    bias_c = rpb[:, :, col_off + WIN_COLS - 1]
    scale = hd ** -0.5

    def one_row(r):
        start = jnp.clip(r - kr // 2, 0, rows - kr)
        k_rows = lax.dynamic_slice_in_dim(kg, start, kr, axis=2)
        v_rows = lax.dynamic_slice_in_dim(vg, start, kr, axis=2)
        k_win = k_rows[:, :, :, col_idx, :]
        v_win = v_rows[:, :, :, col_idx, :]
        q_row = lax.dynamic_index_in_dim(qg, r, axis=2, keepdims=False)
        s = jnp.einsum('bhcd,bhrckd->bhcrk', q_row, k_win).astype(jnp.float32) * scale
        row_off = start + jnp.arange(kr) - r + WIN_ROWS - 1
        b = jnp.take(bias_c, row_off, axis=1).transpose(0, 2, 1, 3)
        s = s + b[None].astype(jnp.float32)
        p = jax.nn.softmax(s.reshape(B, H, GRID_W, kr * WIN_COLS), axis=-1)
        p = p.reshape(B, H, GRID_W, kr, WIN_COLS).astype(v.dtype)
        return jnp.einsum('bhcrk,bhrckd->bhcd', p, v_win)

    out = lax.map(one_row, jnp.arange(rows))
    return out.transpose(1, 2, 0, 3, 4).reshape(B, H, T, hd)


def dilated_group_attention(q, k, v, bias_tab, dil, half_keys):
    B, H, T, hd = q.shape
    L = T // dil
    nb = -(-L // BLK)
    Lp = nb * BLK

    def to_sub(x):
        x = x.reshape(B, H, L, dil, hd).transpose(0, 1, 3, 2, 4)
        return jnp.pad(x, ((0, 0), (0, 0), (0, 0), (0, Lp - L), (0, 0)))

    def key_blocks(x):
        xp = jnp.pad(to_sub(x), ((0, 0), (0, 0), (0, 0), (BLK, BLK), (0, 0)))
        xp = xp.reshape(B, H, dil, nb + 2, BLK, hd)
        return jnp.concatenate([xp[:, :, :, :-2], xp[:, :, :, 1:-1], xp[:, :, :, 2:]], axis=4)

    qs = to_sub(q).reshape(B, H, dil, nb, BLK, hd)
    kb = key_blocks(k)
    vb = key_blocks(v)
    a = np.arange(BLK)[:, None]
    bidx = np.arange(3 * BLK)[None, :]
    rel = bidx - BLK - a
    key_pos = np.arange(nb)[:, None, None] * BLK + bidx[None] - BLK
    valid = (np.abs(rel) <= half_keys)[None] & (key_pos >= 0) & (key_pos < L)
    bias = bias_tab[t5_buckets(rel * dil)].transpose(2, 0, 1)
    s = jnp.einsum('bhrnqd,bhrnkd->bhrnqk', qs, kb).astype(jnp.float32) * (hd ** -0.5)
    s = s + bias[None, :, None, None].astype(jnp.float32)
    s = jnp.where(valid, s, NEG)
    m = jnp.max(s, axis=-1, keepdims=True)
    e = jnp.exp(s - m)
    den = jnp.sum(e, axis=-1, keepdims=True)
    o = jnp.einsum('bhrnqk,bhrnkd->bhrnqd', (e / den).astype(v.dtype), vb)

    def back(t):
        c = t.shape[-1]
        t = t.reshape(B, H, dil, Lp, c)[:, :, :, :L]
        return t.transpose(0, 1, 3, 2, 4).reshape(B, H, T, c)

    return back(o), back(m), back(den)


def encoder_layer(x, norm_mix, w_in, q_norm_a, k_norm_a, q_norm_b, k_norm_b, rpb_a, t5_bias,
                  w_branch_a, w_branch_b, w_out, norm_mlp, w_up, w_down):
    B, T, _ = x.shape
    h = rmsnorm(x, norm_mix)
    proj = h @ w_in
    cuts = list(np.cumsum([QA_W, QA_W, QA_W, QB_W, QB_W, QB_W, D_MODEL]))
    qa, ka, va, qb, kb, vb, ga, gb = jnp.split(proj, cuts, axis=-1)

    def heads(t, n):
        return t.reshape(B, T, n, HEAD_DIM).transpose(0, 2, 1, 3)

    qa = rmsnorm(heads(qa, H_A), q_norm_a)
    ka = rmsnorm(heads(ka, H_A), k_norm_a)
    oa = neighborhood_attention(qa, ka, heads(va, H_A), rpb_a)
    oa = oa.transpose(0, 2, 1, 3).reshape(B, T, QA_W)

    qb = rmsnorm(heads(qb, H_B), q_norm_b)
    kb = rmsnorm(heads(kb, H_B), k_norm_b)
    vb = heads(vb, H_B)
    outs, maxs, dens = [], [], []
    for g in range(N_GROUPS_B):
        sl = slice(g * HG_B, (g + 1) * HG_B)
        o, m, den = dilated_group_attention(qb[:, sl], kb[:, sl], vb[:, sl], t5_bias[:, sl],
                                            DILATIONS_B[g], (WINDOWS_B[g] // 2) // DILATIONS_B[g])
        outs.append(o.astype(jnp.float32)); maxs.append(m); dens.append(den)
    m_all = jnp.max(jnp.stack(maxs, 0), axis=0)
    wts = [d * jnp.exp(m - m_all) for d, m in zip(dens, maxs)]
    ob = sum(w * o for w, o in zip(wts, outs)) / sum(wts)
    ob = ob.astype(x.dtype).transpose(0, 2, 1, 3).reshape(B, T, HG_B * HEAD_DIM)

    merged = jax.nn.sigmoid(ga) * (oa @ w_branch_a) + jax.nn.sigmoid(gb) * (ob @ w_branch_b)
    x = x + merged @ w_out

    hm = rmsnorm(x, norm_mlp)
    u = jax.nn.relu(hm @ w_up)
    return x + (u * u) @ w_down


def setup_inputs(seed: int = 0) -> dict:
    key = jax.random.key(seed)
    ks = jax.random.split(key, 16)
    f32 = jnp.float32

    def nrm(k, shape, scale):
        return jax.random.normal(k, shape, f32) * scale

    return {
        "x_prompt": nrm(ks[0], (BATCH, SEQ, D_MODEL), 1.0),
        "x_sample": nrm(ks[1], (DEC_BATCH, DEC_SEQ, D_MODEL), 1.0),
        "norm_mix": 1.0 + nrm(ks[2], (DEPTH, D_MODEL), 0.02),
        "w_in": nrm(ks[3], (DEPTH, D_MODEL, IN_W), D_MODEL ** -0.5),
        "q_norm_a": 1.0 + nrm(ks[4], (DEPTH, HEAD_DIM), 0.02),
        "k_norm_a": 1.0 + nrm(ks[5], (DEPTH, HEAD_DIM), 0.02),
        "q_norm_b": 1.0 + nrm(ks[6], (DEPTH, HEAD_DIM), 0.02),
        "k_norm_b": 1.0 + nrm(ks[7], (DEPTH, HEAD_DIM), 0.02),
        "rpb_a": nrm(ks[8], (DEPTH, H_A, 2 * WIN_ROWS - 1, 2 * WIN_COLS - 1), 0.1),
        "t5_bias": nrm(ks[9], (NUM_BUCKETS, H_B), 0.1),
        "w_branch_a": nrm(ks[10], (DEPTH, QA_W, D_MODEL), QA_W ** -0.5),
        "w_branch_b": nrm(ks[11], (DEPTH, HG_B * HEAD_DIM, D_MODEL), (HG_B * HEAD_DIM) ** -0.5),
        "w_out": nrm(ks[12], (DEPTH, D_MODEL, D_MODEL), D_MODEL ** -0.5),
        "norm_mlp": 1.0 + nrm(ks[13], (DEPTH, D_MODEL), 0.02),
        "w_up": nrm(ks[14], (DEPTH, D_MODEL, D_FF), D_MODEL ** -0.5),
        "w_down": nrm(ks[15], (DEPTH, D_FF, D_MODEL), D_FF ** -0.5),
    }


def reference(x_prompt, x_sample, norm_mix, w_in, q_norm_a, k_norm_a, q_norm_b, k_norm_b, rpb_a,
              t5_bias, w_branch_a, w_branch_b, w_out, norm_mlp, w_up, w_down):
    y_prompt = x_prompt
    y_sample = x_sample
    for l in range(DEPTH):
        params = (norm_mix[l], w_in[l], q_norm_a[l], k_norm_a[l], q_norm_b[l], k_norm_b[l],
                  rpb_a[l], t5_bias, w_branch_a[l], w_branch_b[l], w_out[l], norm_mlp[l],
                  w_up[l], w_down[l])
        y_prompt = encoder_layer(y_prompt, *params)
        y_sample = encoder_layer(y_sample, *params)
    return (y_prompt, y_sample)
```

```python
import numpy as np
from contextlib import ExitStack
import concourse.bass as bass
import concourse.mybir as mybir
from concourse.bass_utils import run_bass_kernel_spmd

F32 = mybir.dt.float32
BF16 = mybir.dt.bfloat16
AF = mybir.ActivationFunctionType
ALU = mybir.AluOpType
AX = mybir.AxisListType

D = 1024
NEGM = -30000.0
EPS = 1e-6
CH = 512
GRID_W = 64
DILS = (1, 4, 16)
IN_W = 5888
D_FF = 4096
EPOCH = 12000
FILL = 0.5


class _Op:
    __slots__ = ("eng", "fn", "deps", "dma", "token", "idx", "cost", "fb")


class _CostEng:
    def __init__(self, pool=False):
        self.cost = 0.0
        self.k = 2.0 if pool else 1.0

    def then_inc(self, *a, **k):
        return self

    def matmul(self, out, lhsT=None, rhs=None, **kw):
        n = rhs.free_size()
        self.cost += max(n, 64) / 2.2 * (4.0 if rhs.dtype == F32 else 1.0) + 12
        return self

    def transpose(self, out=None, in_=None, identity=None):
        self.cost += 110
        return self

    def activation(self, out=None, in_=None, func=None, accum_out=None, **kw):
        self.cost += 200 + in_.free_size() * 0.62 + (100 if accum_out is not None else 0)
        return self

    def _ew(self, ap, f=1.06):
        self.cost += (110 + ap.free_size() * f) * self.k
        return self

    def tensor_tensor(self, out=None, in0=None, in1=None, op=None):
        return self._ew(in0, 1.15)

    def tensor_scalar(self, out=None, in0=None, **kw):
        return self._ew(in0)

    def scalar_tensor_tensor(self, out=None, in0=None, **kw):
        return self._ew(in0, 1.2 if in0.free_size() > 600 else 1.06)

    def tensor_copy(self, out=None, in_=None):
        return self._ew(in_, 0.9)

    def tensor_reduce(self, out=None, in_=None, **kw):
        return self._ew(in_)

    def reciprocal(self, out=None, in_=None):
        return self._ew(in_, 8.0)

    def memset(self, ap, c):
        return self._ew(ap)

    def affine_select(self, out=None, in_=None, **kw):
        return self._ew(in_)

    def dma_start(self, out=None, in_=None, **kw):
        self.cost += 1800 + out.nbytes() / 220.0
        return self


class Sched:
    ENGS = ("pe", "act", "dve", "pool", "sp")

    def __init__(self):
        self.ops = []
        self.last_w = {}
        self.readers = {}
        self.barrier_deps = set()
        self.last_on_eng = {}
        self.open_dmas = set()
        self.filler_bank = None
        self.make_filler = None
        self.n_fill = 0

    def add(self, eng, fn, reads=(), writes=(), dma=False):
        deps = set(self.barrier_deps)
        for r in reads:
            if r in self.last_w:
                deps.add(self.last_w[r])
        for w in writes:
            if w in self.last_w:
                deps.add(self.last_w[w])
            deps.update(self.readers.get(w, ()))
        op = _Op()
        op.eng, op.fn, op.deps, op.dma = eng, fn, deps, dma
        op.idx = len(self.ops)
        op.token = None
        ce = _CostEng(pool=(eng == "pool"))
        fn(ce)
        op.cost = ce.cost
        op.fb = self.filler_bank
        self.ops.append(op)
        for r in reads:
            self.readers.setdefault(r, []).append(op.idx)
        for w in writes:
            self.last_w[w] = op.idx
            self.readers[w] = []
        self.last_on_eng[eng] = op.idx
        if dma:
            self.open_dmas.add(op.idx)
        return op.idx

    def barrier(self):
        b = set(self.last_on_eng.values()) | set(self.open_dmas)
        self.barrier_deps = b
        self.open_dmas = set()

    def list_schedule(self):
        import heapq
        ops = self.ops
        n = len(ops)
        succ = [[] for _ in range(n)]
        indeg = [0] * n
        for op in ops:
            indeg[op.idx] = len(op.deps)
            for d in op.deps:
                succ[d].append(op.idx)
        fin = [0.0] * n
        est = [0.0] * n
        LAT = 120.0
        future = {e: [] for e in self.ENGS}
        avail = {e: [] for e in self.ENGS}
        free = {e: 0.0 for e in self.ENGS}
        order = {e: [] for e in self.ENGS}
        dma_free = [0.0]
        prev_pe_end = [0.0]
        for op in ops:
            if indeg[op.idx] == 0:
                heapq.heappush(avail[op.eng], op.idx)
        done = 0
        while done < n:
            best = None
            for e in self.ENGS:
                fu = future[e]
                while fu and fu[0][0] <= free[e]:
                    heapq.heappush(avail[e], heapq.heappop(fu)[1])
                if avail[e]:
                    st = free[e]
                elif fu:
                    st = fu[0][0]
                else:
                    continue
                if best is None or st < best[0]:
                    best = (st, e)
            st, e = best
            if not avail[e]:
                free[e] = st
                fu = future[e]
                while fu and fu[0][0] <= free[e]:
                    heapq.heappush(avail[e], heapq.heappop(fu)[1])
            i = heapq.heappop(avail[e])
            op = ops[i]
            if e == "pe" and FILL and op.fb is not None and self.make_filler is not None:
                gap = st - prev_pe_end[0]
                if 250.0 < gap < 20000.0 and prev_pe_end[0] > 0:
                    nf = min(int((gap - 120.0) * FILL / 56.0), 64)
                    if nf > 0:
                        fop = _Op()
                        fop.eng, fop.deps, fop.dma, fop.token, fop.idx = "pe", set(), False, None, -1
                        fop.fn = self.make_filler(op.fb, nf)
                        fop.cost, fop.fb = nf * 56.0, None
                        order["pe"].append(fop)
                        self.n_fill += nf
            if op.dma:
                free[e] = st + 60.0
                t0 = max(st, dma_free[0])
                xfer = max(op.cost - 1800.0, 0.0)
                dma_free[0] = t0 + xfer
                fin[i] = t0 + xfer + 1800.0
            else:
                free[e] = st + op.cost
                fin[i] = free[e]
                if e == "pe":
                    prev_pe_end[0] = free[e]
            order[e].append(op)
            done += 1
            for j in succ[i]:
                indeg[j] -= 1
                if est[j] < fin[i] + LAT:
                    est[j] = fin[i] + LAT
                if indeg[j] == 0:
                    oj = ops[j]
                    if est[j] <= free[oj.eng]:
                        heapq.heappush(avail[oj.eng], j)
                    else:
                        heapq.heappush(future[oj.eng], (est[j], j))
        self.sim_ns = max(fin) if fin else 0.0
        return order

    def emit(self, nc, stack, reorder=True):
        if reorder:
            per_eng = self.list_schedule()
            print("sim_ns", self.sim_ns, "fillers", self.n_fill)
        else:
            per_eng = {e: [] for e in self.ENGS}
            for op in self.ops:
                per_eng[op.eng].append(op)
        n_eng = {e: 0 for e in self.ENGS}
        for op in self.ops:
            if not op.dma:
                n_eng[op.eng] += 1
        sems = {}
        for e in self.ENGS:
            ne = n_eng[e] // EPOCH + 1
            sems[e] = [stack.enter_context(nc.semaphore(f"s_{e}{i}")) for i in range(ne)]
        NDS = {"sp": 40, "pool": 24, "act": 8}
        dsems = {q: [stack.enter_context(nc.semaphore(f"d_{q}{i}")) for i in range(n)]
                 for q, n in NDS.items()}
        cnt = {e: 0 for e in self.ENGS}
        dcnt = {q: 0 for q in NDS}
        dval = {q: [0] * n for q, n in NDS.items()}
        prev_on_sem = {}
        for ename in self.ENGS:
            for op in per_eng[ename]:
                if op.dma:
                    q = op.eng
                    i = dcnt[q] % NDS[q]
                    dcnt[q] += 1
                    dval[q][i] += 16
                    op.token = (("d", q, i), dval[q][i])
                    prev_on_sem[op.idx] = (("d", q, i), dval[q][i] - 16)
                else:
                    k = cnt[op.eng]
                    cnt[op.eng] += 1
                    op.token = (("e", op.eng, k // EPOCH), k % EPOCH + 1)

        def semh(key):
            return dsems[key[1]][key[2]] if key[0] == "d" else sems[key[1]][key[2]]

        ops = self.ops

        def run(ename, e):
            waited = {}

            def wait(key, val):
                if val <= 0:
                    return
                if waited.get(key, 0) >= val:
                    return
                e.wait_ge(semh(key), val)
                waited[key] = val

            for op in per_eng[ename]:
                need = {}
                for d in op.deps:
                    dop = ops[d]
                    if dop.eng == "pe" and ename == "pe" and not dop.dma:
                        continue
                    key, val = dop.token
                    if need.get(key, 0) < val:
                        need[key] = val
                if op.dma:
                    key, val = prev_on_sem[op.idx]
                    if need.get(key, 0) < val:
                        need[key] = val
                for key, val in need.items():
                    wait(key, val)
                ins = op.fn(e)
                key, val = op.token
                ins.then_inc(semh(key), 16 if op.dma else 1)
            if ename in NDS:
                for i in range(NDS[ename]):
                    wait(("d", ename, i), dval[ename][i])

        block = stack.enter_context(nc.Block())

        @block.tensor
        def _(e):
            run("pe", e)

        @block.scalar
        def _(e):
            run("act", e)

        @block.vector
        def _(e):
            run("dve", e)

        @block.gpsimd
        def _(e):
            run("pool", e)

        @block.sync
        def _(e):
            run("sp", e)


class Arena:
    def __init__(self, ap, ncols):
        self.ap = ap
        self.n = ncols
        self.off = 0
        self.peak = 0

    def alloc(self, cols, dtype=F32):
        w = cols if dtype == F32 else (cols + 1) // 2
        w = (w + 7) // 8 * 8
        a = self.ap[:, self.off:self.off + w]
        self.off += w
        self.peak = max(self.peak, self.off)
        assert self.off <= self.n, f"arena overflow {self.off} > {self.n}"
        if dtype != F32:
            a = a.bitcast(dtype)[:, 0:cols]
        return a

    def mark(self):
        return self.off

    def reset(self, m):
        self.off = m


def _t5_buckets(rel):
    half = 16
    ret = np.where(rel > 0, half, 0)
    n = np.abs(rel)
    max_exact = half // 2
    large = max_exact + (np.log(np.maximum(n, 1) / max_exact)
                         / np.log(1024 / max_exact) * (half - max_exact)).astype(np.int32)
    large = np.minimum(large, half - 1)
    return (ret + np.where(n < max_exact, n, large)).astype(np.int32)


def _tables(rpb, t5):
    kc = np.arange(64)[:, None]
    qc = np.arange(64)[None, :]
    cs = np.clip(qc - 8, 0, 48)
    cval = (kc >= cs) & (kc < cs + 16)
    cidx = np.clip(kc - qc + 15, 0, 30)
    ga = np.zeros((128, 8, 14, 64), np.float32)
    ma = np.zeros((128, 8, 14, 64), np.float32)
    for blk in range(14):
        d = blk - 7
        for kr2 in range(2):
            dr = d + kr2
            vals = rpb[:, dr + 7, :][:, cidx]
            vals = np.where(cval[None], vals, 0.0)
            ga[kr2 * 64:(kr2 + 1) * 64, :, blk, :] = vals.transpose(1, 0, 2)
            ma[kr2 * 64:(kr2 + 1) * 64, :, blk, :] = np.where(cval, 0.0, NEGM)[:, None, :]
    gb, mb = [], []
    pk = np.arange(128)[:, None]
    pq = np.arange(128)[None, :]
    for g in range(3):
        dil = DILS[g]
        nk = 3 if g < 2 else 5
        G = np.zeros((128, 4, nk, 128), np.float32)
        M = np.zeros((128, 4, nk, 128), np.float32)
        for kt in range(nk):
            if g < 2:
                dpos = 128 * (kt - 1) + pk - pq
                valid = np.abs(dpos) <= 64
            else:
                rk, lk = pk // 32, pk % 32
                rq, lq = pq // 32, pq % 32
                dpos = 32 * (kt - 2) + lk - lq
                valid = (rk == rq) & (np.abs(dpos) <= 64)
            bk = _t5_buckets(dpos * dil)
            for h in range(4):
                G[:, h, kt, :] = np.where(valid, t5[bk, 4 * g + h], 0.0)
                M[:, h, kt, :] = np.where(valid, 0.0, NEGM)
        gb.append(G)
        mb.append(M)
    return ga, ma, gb, mb


def build_program(T_list, debug=False, upto=99, max_ops=None, half=None):
    half = set(half or ())
    nc = bass.Bass("TRN2", target_bir_lowering=False)
    NS = len(T_list)

    def din(name, shape, dt=F32):
        return nc.dram_tensor(name, list(shape), dt, kind="ExternalInput").ap()

    def dscr(name, shape, dt=BF16):
        kind = "ExternalOutput" if debug else "Internal"
        return nc.dram_tensor(name, list(shape), dt, kind=kind).ap()

    xs = [din(f"x{i}", (T, D)) for i, T in enumerate(T_list)]
    ys = [nc.dram_tensor(f"y{i}", [(4 * CH if i in half else T), D], F32, kind="ExternalOutput").ap() for i, T in enumerate(T_list)]
    slot1 = din("slot1", (128, 8)) if half else None
    tS = din("tS", (28, 128, 768)) if half else None
    w_in = din("w_in", (D, IN_W))
    w_a = din("w_a", (512, D))
    w_b = din("w_b", (256, D))
    w_o = din("w_o", (D, D))
    w_up = din("w_up", (D, D_FF))
    w_dn = din("w_dn", (D_FF, D))
    g_mix = din("g_mix", (1, D))
    g_mlp = din("g_mlp", (1, D))
    g_qa = din("g_qa", (1, 64))
    g_ka = din("g_ka", (1, 64))
    g_qb = din("g_qb", (1, 64))
    g_kb = din("g_kb", (1, 64))
    tA_g = din("tA_g", (128, 8 * 14 * 64))
    tA_m = din("tA_m", (128, 8 * 14 * 64))
    NKB = (3, 3, 5)
    tB_g = [din(f"tB_g{g}", (128, 4 * NKB[g] * 128)) for g in range(3)]
    tB_m = [din(f"tB_m{g}", (128, 4 * NKB[g] * 128)) for g in range(3)]

    WIN = nc.dram_tensor("WINb", [D, IN_W], BF16, kind="Internal").ap()
    WA = nc.dram_tensor("WAb", [512, D], BF16, kind="Internal").ap()
    WB = nc.dram_tensor("WBb", [256, D], BF16, kind="Internal").ap()
    WO = nc.dram_tensor("WOb", [D, D], BF16, kind="Internal").ap()
    WUP = nc.dram_tensor("WUPb", [D, D_FF], BF16, kind="Internal").ap()
    WDN = nc.dram_tensor("WDNb", [D_FF, D], BF16, kind="Internal").ap()
    KAT = [dscr(f"KAT{i}", (4, 128, T)) for i, T in enumerate(T_list)]
    VA = [dscr(f"VA{i}", (T, 8 * 65)) for i, T in enumerate(T_list)]
    KBT = [dscr(f"KBT{i}", (3, 2, 128, T)) for i, T in enumerate(T_list)]
    VB = [dscr(f"VB{i}", (3, T, 4 * 65)) for i, T in enumerate(T_list)]
    OAT = [dscr(f"OAT{i}", (4, 128, T)) for i, T in enumerate(T_list)]
    OBT = [dscr(f"OBT{i}", (2, 128, T)) for i, T in enumerate(T_list)]
    HTD = [nc.dram_tensor(f"HTD{i}", [128, 8, T], BF16, kind="Internal").ap() for i, T in enumerate(T_list)]
    RDS = nc.dram_tensor("RDS", [8, CH], F32, kind="Internal").ap()

    S = Sched()
    stack = ExitStack()
    dbg_seen = set()

    def dbg_dump(name, ap, res, dt=BF16):
        if not debug or name in dbg_seen:
            return
        dbg_seen.add(name)
        t = nc.dram_tensor("DBG_" + name, [ap.shape[0], ap.shape[1]], dt, kind="ExternalOutput").ap()
        S.add("pool", lambda e: e.dma_start(out=t, in_=ap), reads=res, writes=[("dbg", name)], dma=True)
    ARENA_COLS = 207 * 256
    arena_t = stack.enter_context(nc.sbuf_tensor("arena", [128, ARENA_COLS], F32))
    ps_t = stack.enter_context(nc.psum_tensor("ps", [128, 4096], F32))
    A = Arena(arena_t, ARENA_COLS)

    def bank(b, n=1):
        return ps_t[:, 512 * b:512 * (b + n)]

    def bankb(b):
        return ps_t[:, 512 * b:512 * (b + 1)].bitcast(BF16)

    def PB(b):
        return ("ps", b)

    ident = A.alloc(128, BF16)
    ones_f = A.alloc(64, F32)
    epst = A.alloc(8, F32)
    gmix = A.alloc(D, F32)
    graw = A.alloc(4 * 64, F32)
    gq_a = A.alloc(512, F32)
    gk_a = A.alloc(512, F32)
    gq_b = A.alloc(512, F32)
    gk_b = A.alloc(512, F32)
    junk = A.alloc(D, BF16)
    hbuf = {}
    hT = A.alloc(8 * CH, BF16)
    hTb = A.alloc(8 * CH, BF16)
    stats = A.alloc(64, F32)
    nst = A.alloc(3 * 8 * 2, F32)
    sqb = [A.alloc(512, F32) for _ in range(2)]
    tmpb = [A.alloc(512, F32) for _ in range(2)]
    hT_v = hT.rearrange("p (k t) -> p k t", k=8)
    hT2_v = [hT_v, hTb.rearrange("p (k t) -> p k t", k=8)]

    def Hload(si, c):
        sl = c % 2
        t0 = c * CH
        S.add("sp", lambda e: e.dma_start(out=hT2_v[sl], in_=HTD[si][:, :, t0:t0 + CH]),
              reads=[("HTD", si, c)], writes=[("hT", sl, t) for t in range(4)], dma=True)
        return hT2_v[sl], [("hT", sl, t) for t in range(4)]
    slot1_sb = A.alloc(8, F32)
    tabA = A.alloc(8 * 14 * 64, BF16)
    tabA_v = tabA.rearrange("p (h b q) -> p h b q", h=8, b=14)
    tabB = [A.alloc(4 * NKB[g] * 128, BF16) for g in range(3)]
    tabB_v = [tabB[g].rearrange("p (h k q) -> p h k q", h=4, k=NKB[g]) for g in range(3)]
    tstg = [A.alloc(768, F32) for _ in range(4)]
    PERM_MARK = A.mark()

    def weight_casts(after):
        for (c0, c1, nm) in ((0, 512, "WIN"), (1536, 2304, "WIN"), (3840, 5888, "WIN"), (512, 1536, "WINKV"), (2304, 3840, "WINKV")):
            for r0 in range(0, D, 512):
                S.add("pool", lambda e, c0=c0, c1=c1, r0=r0: e.dma_start(out=WIN[r0:r0 + 512, c0:c1], in_=w_in[r0:r0 + 512, c0:c1]),
                      reads=after, writes=[(nm, c0, r0)], dma=True)
        for nm, src, dst, rows in (("WA", w_a, WA, 512), ("WB", w_b, WB, 256),
                                   ("WO", w_o, WO, 512), ("WUP", w_up, WUP, 256), ("WDN", w_dn, WDN, 1024)):
            R = src.shape[0]
            for r0 in range(0, R, rows):
                S.add("pool", lambda e, s=src, d=dst, r0=r0, rows=rows: e.dma_start(out=d[r0:r0 + rows, :], in_=s[r0:r0 + rows, :]),
                      reads=after, writes=[(nm, r0)], dma=True)

    def setup():
        S.add("pool", lambda e: e.memset(ident, 0.0), writes=["ident"])
        S.add("pool", lambda e: e.affine_select(out=ident, in_=ident, pattern=[[-1, 128]], compare_op=ALU.not_equal,
                                                fill=1.0, base=0, channel_multiplier=1),
              reads=["ident"], writes=["ident"])
        S.add("dve", lambda e: e.memset(ones_f, 1.0), writes=["ones_f"])
        S.add("dve", lambda e: e.memset(epst, EPS), writes=["eps"])
        pass
        S.add("sp", lambda e: e.dma_start(out=gmix, in_=g_mix[0].partition_broadcast(128)), writes=["gmix"], dma=True)
        for i, gsrc in enumerate((g_qa, g_ka, g_qb, g_kb)):
            S.add("sp", lambda e, i=i, gsrc=gsrc: e.dma_start(out=graw[:, 64 * i:64 * i + 64], in_=gsrc[0].partition_broadcast(128)),
                  writes=[("graw", i)], dma=True)
        for i, dst in enumerate((gq_a, gk_a, gq_b, gk_b)):
            def f(e, i=i, dst=dst):
                return e.tensor_copy(out=dst.rearrange("p (h d) -> p h d", h=8),
                                     in_=graw[:, 64 * i:64 * i + 64].unsqueeze(1).to_broadcast([128, 8, 64]))
            S.add("dve", f, reads=[("graw", i)], writes=[("gt", i)])
        if half:
            S.add("sp", lambda e: e.dma_start(out=slot1_sb, in_=slot1), writes=["slot1"], dma=True)
        pc = 0
        for (gsrc, msrc, dst, n, nm) in [(tA_g, tA_m, tabA, 8 * 14 * 64, "tabA")] + \
                [(tB_g[g], tB_m[g], tabB[g], 4 * NKB[g] * 128, f"tabB{g}") for g in range(3)]:
            for p0 in range(0, n, 768):
                w = min(768, n - p0)
                sa, sb2 = tstg[2 * (pc % 2)], tstg[2 * (pc % 2) + 1]
                ra, rb2 = ("tstg", 2 * (pc % 2)), ("tstg", 2 * (pc % 2) + 1)
                pc += 1
                S.add("sp", lambda e, gsrc=gsrc, p0=p0, w=w, sa=sa: e.dma_start(out=sa[:, 0:w], in_=gsrc[:, p0:p0 + w]), writes=[ra], dma=True)
                S.add("sp", lambda e, msrc=msrc, p0=p0, w=w, sb2=sb2: e.dma_start(out=sb2[:, 0:w], in_=msrc[:, p0:p0 + w]), writes=[rb2], dma=True)
                S.add("dve", lambda e, dst=dst, p0=p0, w=w, sa=sa, sb2=sb2: e.tensor_tensor(out=dst[:, p0:p0 + w], in0=sa[:, 0:w], in1=sb2[:, 0:w], op=ALU.add),
                      reads=[ra, rb2], writes=[nm])
        for i, dst in ((0, gq_a), (2, gq_b)):
            S.add("dve", lambda e, dst=dst: e.tensor_scalar(out=dst, in0=dst, scalar1=0.125, scalar2=None, op0=ALU.mult),
                  reads=[("gt", i)], writes=[("gt", i)])

    setup()
    WRES = {"WIN": [("WIN", c0, r0) for c0 in (0, 1536, 3840) for r0 in (0, 512)],
            "WINKV": [("WINKV", c0, r0) for c0 in (512, 2304) for r0 in (0, 512)], "WA": [("WA", 0)], "WB": [("WB", 0)],
            "WO": [("WO", r0) for r0 in range(0, D, 512)], "WUP": [("WUP", r0) for r0 in range(0, D, 256)],
            "WDN": [("WDN", r0) for r0 in range(0, D_FF, 1024)]}

    def make_filler(fb, nf):
        def f(e):
            for _ in range(nf):
                ins = e.matmul(bank(fb)[:, 0:128], lhsT=ident, rhs=ident, start=True, stop=True)
            return ins
        return f
    S.make_filler = make_filler

    tile_ctr = [0]

    def H(x_ap, c, tag):
        for t in range(4):
            i = tile_ctr[0]
            tile_ctr[0] += 1
            s2 = i % 2
            s4 = i % 4
            xtile = hbuf['xt'][s2]
            hb = hbuf['hb']
            r0 = c * CH + t * 128
            S.add("sp", lambda e, xtile=xtile, r0=r0: e.dma_start(out=xtile, in_=x_ap[r0:r0 + 128, :]),
                  writes=[("xt", s2)], dma=True)
            ss = stats[:, s4:s4 + 1]
            sd = stats[:, 4 + s4:5 + s4]
            rs = stats[:, 8 + s4:9 + s4]
            S.add("act", lambda e, xtile=xtile, ss=ss: e.activation(out=junk, in_=xtile, func=AF.Square, accum_out=ss),
                  reads=[("xt", s2)], writes=["junk", ("ss", s4)])
            S.add("act", lambda e, ss=ss, sd=sd: e.activation(out=sd, in_=ss, func=AF.Ln, scale=1.0 / D, bias=epst[:, 0:1]),
                  reads=[("ss", s4), "eps"], writes=[("sd", s4)])
            S.add("act", lambda e, sd=sd, rs=rs: e.activation(out=rs, in_=sd, func=AF.Exp, scale=-0.5), reads=[("sd", s4)], writes=[("rs", s4)])
            S.add("dve", lambda e, xtile=xtile, rs=rs, s2=s2: e.scalar_tensor_tensor(
                out=hb[s2], in0=xtile, scalar=rs, in1=gmix, op0=ALU.mult, op1=ALU.mult),
                reads=[("xt", s2), ("rs", s4), "gmix"], writes=[("hb", s2)])
            tb = 6 + s2
            pst = bankb(tb)

            def f(e, s2=s2, pst=pst):
                for k in range(8):
                    ins = e.transpose(out=pst[:, 128 * k:128 * k + 128], in_=hb[s2][:, 128 * k:128 * k + 128], identity=ident)
                return ins
            S.add("pe", f, reads=[("hb", s2), "ident"], writes=[PB(tb)])
            S.add("act", lambda e, pst=pst, t=t: e.activation(
                out=hT_v[:, :, 128 * t:128 * t + 128], in_=pst.rearrange("p (k t) -> p k t", k=8), func=AF.Copy),
                reads=[PB(tb)], writes=[("hT", 0, t)])

    nslot = [0]

    def headnorm(src, src_res, nh, gain, gain_res, out, out_res):
        sl = nslot[0] % 2
        nslot[0] += 1
        W = nh * 64
        sq = sqb[sl][:, 0:W]
        tmp = tmpb[sl][:, 0:W]
        ssq = nst[:, 24 * sl:24 * sl + nh]
        sd = nst[:, 24 * sl + 8:24 * sl + 8 + nh]
        rs = nst[:, 24 * sl + 16:24 * sl + 16 + nh]
        S.add("act", lambda e: e.activation(out=sq, in_=src, func=AF.Square), reads=[src_res], writes=[("sq", sl)])
        S.add("dve", lambda e: e.tensor_reduce(out=ssq, in_=sq.rearrange("p (h d) -> p h d", h=nh), axis=AX.X, op=ALU.add),
              reads=[("sq", sl)], writes=[("nssq", sl)])
        S.add("act", lambda e: e.activation(out=sd, in_=ssq, func=AF.Ln, scale=1.0 / 64, bias=epst[:, 0:1]),
              reads=[("nssq", sl), "eps"], writes=[("nsd", sl)])
        S.add("act", lambda e: e.activation(out=rs, in_=sd, func=AF.Exp, scale=-0.5), reads=[("nsd", sl)], writes=[("nrs", sl)])
        S.add("dve", lambda e: e.tensor_tensor(out=tmp.rearrange("p (h d) -> p h d", h=nh),
                                               in0=src.rearrange("p (h d) -> p h d", h=nh),
                                               in1=rs.unsqueeze(2).to_broadcast([128, nh, 64]), op=ALU.mult),
              reads=[src_res, ("nrs", sl)], writes=[("tmp", sl)])
        S.add("dve", lambda e: e.tensor_tensor(out=out, in0=tmp, in1=gain[:, 0:W], op=ALU.mult),
              reads=[("tmp", sl), gain_res], writes=[out_res])

    def transposes(src, src_res, n, tb):
        pst = bankb(tb)

        def f(e):
            for k in range(n):
                ins = e.transpose(out=pst[:, 128 * k:128 * k + 128], in_=src[:, 128 * k:128 * k + 128], identity=ident)
            return ins
        S.add("pe", f, reads=[src_res, "ident"], writes=[PB(tb)])
        return pst

    def pass1(si):
        T = T_list[si]
        NC_ = T // CH
        S.filler_bank = 0
        A.reset(PERM_MARK)
        hbuf['xt'] = [A.alloc(D, F32) for _ in range(2)]
        hbuf['hb'] = [A.alloc(D, BF16) for _ in range(2)]
        wkv = A.alloc(8 * 2560, BF16)
        wkv_v = wkv.rearrange("p (k n) -> p k n", k=8)
        hT3 = A.alloc(8 * CH, BF16)
        hT3_v = hT3.rearrange("p (k t) -> p k t", k=8)
        kn = [A.alloc(512, BF16) for _ in range(2)]
        kst = [A.alloc(4 * CH, BF16) for _ in range(2)]
        vst = [A.alloc(4 * 8 * 65, BF16) for _ in range(2)]
        kbst = [[A.alloc(2 * CH, BF16) for _ in range(3)] for _ in range(2)]
        vbst = [[A.alloc(4 * 4 * 65, BF16) for _ in range(3)] for _ in range(2)]
        segs = [(512, 512), (1024, 512), (2304, 256), (3072, 256), (2560, 256), (2816, 256), (3328, 256), (3584, 256)]
        off = 0
        WINv = WIN.rearrange("(k p) n -> p k n", p=128)
        if si == 0:
            w_in_v = w_in.rearrange("(k p) n -> p k n", p=128)
            wstg = [A.alloc(8 * 256, F32) for _ in range(2)]
            pi = 0
            for (c0, w) in segs:
                for q0 in range(0, w, 256):
                    st = wstg[pi % 2].rearrange("p (k n) -> p k n", k=8)
                    rs_ = ("wstg", pi % 2)
                    S.add("sp", lambda e, st=st, c0=c0, q0=q0: e.dma_start(out=st, in_=w_in_v[:, :, c0 + q0:c0 + q0 + 256]),
                          writes=[rs_], dma=True)
                    dstv = wkv_v[:, :, off + q0:off + q0 + 256]
                    if pi % 2 == 0:
                        S.add("dve", lambda e, st=st, dstv=dstv: e.tensor_copy(out=dstv, in_=st), reads=[rs_], writes=[("wkv", off)])
                    else:
                        S.add("act", lambda e, st=st, dstv=dstv: e.activation(out=dstv, in_=st, func=AF.Copy), reads=[rs_], writes=[("wkv", off)])
                    pi += 1
                off += w
            weight_casts([("wstg", 0), ("wstg", 1)])
        else:
            for (c0, w) in segs:
                S.add("sp", lambda e, off=off, c0=c0, w=w: e.dma_start(out=wkv_v[:, :, off:off + w], in_=WINv[:, :, c0:c0 + w]),
                      reads=WRES["WINKV"], writes=[("wkv", off)], dma=True)
                off += w
        wkv_res = [("wkv", o) for o in (0, 512, 1024, 1280, 1536, 1792, 2048, 2304)]
        for p in range(2):
            S.add("dve", lambda e, p=p: e.memset(vst[p], 1.0), writes=[("vst", p)])
            for g in range(3):
                S.add("pool", lambda e, p=p, g=g: e.memset(vbst[p][g], 1.0), writes=[("vbst", p, g)])
        gctr = [0]

        def proj(lhs_fn, lhs_res, col0, ncol, bcol0=0, b=None, wres=()):
            if b is None:
                b = 4 + gctr[0] % 2
                gctr[0] += 1
            bk = bank(b)

            def f(e):
                for k in range(8):
                    ins = e.matmul(bk[:, bcol0:bcol0 + ncol], lhsT=lhs_fn(k), rhs=wkv_v[:, k, col0:col0 + ncol],
                                   start=(k == 0), stop=(k == 7))
                return ins
            S.add("pe", f, reads=list(lhs_res) + list(wres), writes=[PB(b)])
            return b

        for c in range(NC_):
            par = c % 2
            if si in half:
                S.add("dve", lambda e, par=par, c=c: e.tensor_copy(
                    out=vst[par].rearrange("p (n e) -> p n e", e=65)[:, :, 64:65],
                    in_=slot1_sb[:, c:c + 1].unsqueeze(1).to_broadcast([128, 32, 1])),
                    reads=["slot1"], writes=[("vst", par)])
                for g in range(3):
                    S.add("dve", lambda e, par=par, c=c, g=g: e.tensor_copy(
                        out=vbst[par][g].rearrange("p (n e) -> p n e", e=65)[:, :, 64:65],
                        in_=slot1_sb[:, c:c + 1].unsqueeze(1).to_broadcast([128, 16, 1])),
                        reads=["slot1"], writes=[("vbst", par, g)])
            H(xs[si], c, "p1")
            hres = [("hT", 0, t) for t in range(4)]
            S.add("sp", lambda e, c=c: e.dma_start(out=HTD[si][:, :, c * CH:c * CH + CH], in_=hT_v),
                  reads=hres, writes=[("HTD", si, c)], dma=True)
            for m in range(4):
                S.add("pool", lambda e, m=m: e.tensor_copy(
                    out=hT3_v[:, :, 128 * m:128 * m + 128].rearrange("p k (r l) -> p k r l", r=4),
                    in_=hT_v.rearrange("p k (l m r) -> p k m r l", m=4, r=4)[:, :, m]),
                    reads=hres, writes=[("hT3", m)])
            for t in range(4):
                lf = lambda k, t=t: hT_v[:, k, 128 * t:128 * t + 128]
                b = proj(lf, [("hT", 0, t)], 0, 512, wres=[wkv_res[0]])
                sl = t % 2
                headnorm(bank(b), PB(b), 8, gk_a, ("gt", 1), kn[sl], ("kn", sl))
                tb = 6 + t % 2
                pst = transposes(kn[sl], ("kn", sl), 4, tb)
                S.add("act", lambda e, pst=pst, t=t, par=par: e.activation(
                    out=kst[par].rearrange("p (q t) -> p q t", q=4)[:, :, 128 * t:128 * t + 128],
                    in_=pst[:, 0:512].rearrange("p (q t) -> p q t", q=4), func=AF.Copy),
                    reads=[PB(tb)], writes=[("kst", par)])
                b = proj(lf, [("hT", 0, t)], 512, 512, wres=[wkv_res[1]])
                S.add("act", lambda e, b=b, t=t, par=par: e.activation(
                    out=vst[par].rearrange("p (t h e) -> p t h e", t=4, h=8)[:, t, :, 0:64],
                    in_=bank(b).rearrange("p (h e) -> p h e", h=8), func=AF.Copy),
                    reads=[PB(b)], writes=[("vst", par)])
                b = proj(lf, [("hT", 0, t)], 1024, 512, wres=[wkv_res[2], wkv_res[3]])
                sl = t % 2
                headnorm(bank(b)[:, 0:256], PB(b), 4, gk_b, ("gt", 3), kn[sl][:, 0:256], ("kn", sl))
                pst = transposes(kn[sl], ("kn", sl), 2, tb)
                S.add("act", lambda e, pst=pst, t=t, par=par: e.activation(
                    out=kbst[par][0].rearrange("p (q t) -> p q t", q=2)[:, :, 128 * t:128 * t + 128],
                    in_=pst[:, 0:256].rearrange("p (q t) -> p q t", q=2), func=AF.Copy),
                    reads=[PB(tb)], writes=[("kbst", par, 0)])
                S.add("act", lambda e, b=b, t=t, par=par: e.activation(
                    out=vbst[par][0].rearrange("p (t h e) -> p t h e", t=4, h=4)[:, t, :, 0:64],
                    in_=bank(b)[:, 256:512].rearrange("p (h e) -> p h e", h=4), func=AF.Copy),
                    reads=[PB(b)], writes=[("vbst", par, 0)])
                b = proj(lf, [("hT", 0, t)], 1536, 512, wres=[wkv_res[4], wkv_res[5]])
                headnorm(bank(b), PB(b), 8, gk_b, ("gt", 3), kn[sl], ("kn", sl))
                pst = transposes(kn[sl], ("kn", sl), 4, tb)
                S.add("act", lambda e, pst=pst, t=t, par=par: e.activation(
                    out=kbst[par][1].rearrange("p (q m L) -> p q m L", q=2, m=4)[:, :, :, 32 * t:32 * t + 32],
                    in_=pst[:, 0:256].rearrange("p (q l m) -> p q m l", q=2, m=4), func=AF.Copy),
                    reads=[PB(tb)], writes=[("kbst", par, 1)])
                for q in range(2):
                    S.add("act", lambda e, pst=pst, t=t, par=par, q=q: e.activation(
                        out=kbst[par][2][:, CH * q:CH * q + CH].rearrange("p (m r L) -> p m r L", m=4, r=4)[:, :, :, 8 * t:8 * t + 8],
                        in_=pst[:, 256 + 128 * q:384 + 128 * q].rearrange("p (l m r) -> p m r l", m=4, r=4), func=AF.Copy),
                        reads=[PB(tb)], writes=[("kbst", par, 2)])
            for m in range(4):
                b = 4 + gctr[0] % 2
                gctr[0] += 1
                proj(lambda k, m=m: hT_v[:, k, :].rearrange("p (l m) -> p m l", m=4)[:, m, :], hres, 2048, 256, 0, b,
                     wres=[wkv_res[6]])
                proj(lambda k, m=m: hT3_v[:, k, 128 * m:128 * m + 128], [("hT3", m)], 2304, 256, 256, b, wres=[wkv_res[7]])
                for g in (1, 2):
                    S.add("act", lambda e, b=b, m=m, par=par, g=g: e.activation(
                        out=vbst[par][g].rearrange("p (t h e) -> p t h e", t=4, h=4)[:, m, :, 0:64],
                        in_=bank(b)[:, 256 * (g - 1):256 * g].rearrange("p (h e) -> p h e", h=4), func=AF.Copy),
                        reads=[PB(b)], writes=[("vbst", par, g)])
            t0 = c * CH
            S.add("pool", lambda e, par=par, t0=t0: e.dma_start(
                out=KAT[si][:, :, t0:t0 + CH].rearrange("q p t -> p q t"),
                in_=kst[par].rearrange("p (q t) -> p q t", q=4)),
                reads=[("kst", par)], writes=[("KAT", si, c)], dma=True)
            S.add("pool", lambda e, par=par, t0=t0: e.dma_start(
                out=VA[si][t0:t0 + CH, :].rearrange("(t p) n -> p t n", p=128),
                in_=vst[par].rearrange("p (t n) -> p t n", t=4)),
                reads=[("vst", par)], writes=[("VA", si, c)], dma=True)
            for g in range(3):
                S.add("pool", lambda e, par=par, t0=t0, g=g: e.dma_start(
                    out=KBT[si][g, :, :, t0:t0 + CH].rearrange("q p t -> p q t"),
                    in_=kbst[par][g].rearrange("p (q t) -> p q t", q=2)),
                    reads=[("kbst", par, g)], writes=[("KBT", si, g, c)], dma=True)
                S.add("pool", lambda e, par=par, t0=t0, g=g: e.dma_start(
                    out=VB[si][g, t0:t0 + CH, :].rearrange("(t p) n -> p t n", p=128),
                    in_=vbst[par][g].rearrange("p (t n) -> p t n", t=4)),
                    reads=[("vbst", par, g)], writes=[("VB", si, g, c)], dma=True)

    def pass2(si):
        T = T_list[si]
        NC_ = T // CH
        R = T // GRID_W
        NT = T // 128
        S.filler_bank = 6
        A.reset(PERM_MARK)
        wq = A.alloc(8 * 1280, BF16)
        wq_v = wq.rearrange("p (k n) -> p k n", k=8)
        qn = [A.alloc(512, BF16) for _ in range(2)]
        qaT = A.alloc(4 * CH, BF16)
        qaT_v = qaT.rearrange("p (q t) -> p q t", q=4)
        qbT = [A.alloc(2 * CH, BF16) for _ in range(3)]
        qbT_v = [qbT[g].rearrange("p (q t) -> p q t", q=2) for g in range(3)]
        kaw = A.alloc(4 * 16 * 64, BF16)
        kaw_v = kaw.rearrange("p (q t) -> p q t", q=4)
        vae = A.alloc(8 * 520, BF16)
        vao = A.alloc(7 * 520, BF16)
        vae_v = vae.rearrange("p (n c) -> p n c", c=520)
        vao_v = vao.rearrange("p (n c) -> p n c", c=520)
        kbw0 = A.alloc(2 * 6 * 128, BF16)
        kbw0_v = kbw0.rearrange("p (q t) -> p q t", q=2)
        vbw0 = A.alloc(6 * 260, BF16)
        vbw0_v = vbw0.rearrange("p (n c) -> p n c", c=260)
        kbwm = [None] + [[A.alloc(2 * NKB[g] * 128, BF16) for _ in range(2)] for g in (1, 2)]
        vbwm = [None] + [[A.alloc(NKB[g] * 260, BF16) for _ in range(2)] for g in (1, 2)]
        sbb = [A.alloc(768, F32) for _ in range(2)]
        ptb = [A.alloc(768, BF16) for _ in range(2)]
        tot = A.alloc(4 * CH, F32)
        tot_v = tot.rearrange("p (h t) -> p h t", h=4)
        oas = [A.alloc(CH, F32) for _ in range(4)]
        lnd = A.alloc(CH, F32)
        rdb = [A.alloc(CH, F32) for _ in range(4)]
        bcs = [A.alloc(CH, F32) for _ in range(4)]
        ost = A.alloc(4 * CH, BF16)
        ost_v = ost.rearrange("p (q t) -> p q t", q=4)
        obst = A.alloc(2 * CH, BF16)
        obst_v = obst.rearrange("p (q t) -> p q t", q=2)

        WINv = WIN.rearrange("(k p) n -> p k n", p=128)
        S.add("sp", lambda e: e.dma_start(out=wq_v[:, :, 0:512], in_=WINv[:, :, 0:512]), reads=WRES["WIN"], writes=[("wq", 0)], dma=True)
        S.add("sp", lambda e: e.dma_start(out=wq_v[:, :, 512:1280], in_=WINv[:, :, 1536:2304]), reads=WRES["WIN"], writes=[("wq", 1)], dma=True)
        sctr = [0]
        octr = [0]

        rctr = [0]

        def normalize(src_ap, src_res, dst_ap, dst_res):
            src_rl = src_res if isinstance(src_res, list) else [src_res]
            ri = rctr[0] % 4
            rctr[0] += 1
            S.add("act", lambda e: e.activation(out=lnd[64:65, :], in_=src_ap[64:65, :], func=AF.Ln),
                  reads=src_rl, writes=["lnd"])
            S.add("act", lambda e: e.activation(out=rdb[ri][64:65, :], in_=lnd[64:65, :], func=AF.Exp, scale=-1.0),
                  reads=["lnd"], writes=[("rd", ri)])
            S.add("sp", lambda e: e.dma_start(out=RDS[ri:ri + 1, :], in_=rdb[ri][64:65, :]),
                  reads=[("rd", ri)], writes=[("RDS", ri)], dma=True)
            S.add("sp", lambda e: e.dma_start(out=bcs[ri][0:64, :], in_=RDS[ri].partition_broadcast(64)),
                  reads=[("RDS", ri)], writes=[("bcs", ri)], dma=True)
            S.add("dve", lambda e: e.tensor_tensor(out=dst_ap, in0=src_ap[0:64, :], in1=bcs[ri][0:64, :], op=ALU.mult),
                  reads=src_rl + [("bcs", ri)], writes=[dst_res])

        gctr = [0]
        is_half = si in half
        c_rng = range(2, 6) if is_half else range(NC_)
        tsb = [A.alloc(768, BF16) for _ in range(2)] if is_half else None
        tsctr = [0]
        for c in c_rng:
            hTc, hres2 = Hload(si, c)
            lo = max(8 * c - 4, 0)
            hi = min(8 * c + (12 if is_half else 11), R)
            nrow = hi - lo
            S.add("sp", lambda e, lo=lo, nrow=nrow: e.dma_start(
                out=kaw_v[:, :, 0:nrow * 64], in_=KAT[si][:, :, lo * 64:(lo + nrow) * 64].rearrange("q p t -> p q t")),
                reads=[("KAT", si, cc) for cc in range(max(c - 1, 0), min(c + 2, NC_))], writes=["kaw"], dma=True)
            n_e = (nrow + 1) // 2
            if lo + 2 * n_e > R:
                n_e -= 1
            n_o = (hi - (lo + 1)) // 2
            S.add("sp", lambda e, lo=lo, n_e=n_e: e.dma_start(
                out=vae_v[:, 0:n_e, :], in_=VA[si][lo * 64:lo * 64 + n_e * 128, :].rearrange("(n p) c -> p n c", p=128)),
                reads=[("VA", si, cc) for cc in range(max(c - 1, 0), min(c + 2, NC_))], writes=["vae"], dma=True)
            S.add("sp", lambda e, lo=lo, n_o=n_o: e.dma_start(
                out=vao_v[:, 0:n_o, :], in_=VA[si][lo * 64 + 64:lo * 64 + 64 + n_o * 128, :].rearrange("(n p) c -> p n c", p=128)),
                reads=[("VA", si, cc) for cc in range(max(c - 1, 0), min(c + 2, NC_))], writes=["vao"], dma=True)
            t_lo1, t_hi1 = max(4 * c - 1, 0), min(4 * c + 5, NT)
            nt1 = t_hi1 - t_lo1
            cr1 = list(range(max(c - 1, 0), min(c + 2, NC_)))
            S.add("sp", lambda e, t_lo1=t_lo1, nt1=nt1: e.dma_start(
                out=kbw0_v[:, :, 0:nt1 * 128], in_=KBT[si][0, :, :, t_lo1 * 128:(t_lo1 + nt1) * 128].rearrange("q p t -> p q t")),
                reads=[("KBT", si, 0, cc) for cc in cr1], writes=["kbw0"], dma=True)
            S.add("sp", lambda e, t_lo1=t_lo1, nt1=nt1: e.dma_start(
                out=vbw0_v[:, 0:nt1, :], in_=VB[si][0, t_lo1 * 128:(t_lo1 + nt1) * 128, :].rearrange("(n p) c -> p n c", p=128)),
                reads=[("VB", si, 0, cc) for cc in cr1], writes=["vbw0"], dma=True)

            def qproj(t, col0, ncol, hTc=hTc, hres2=hres2):
                b = 4 + gctr[0] % 2
                gctr[0] += 1
                bk = bank(b)

                def f(e):
                    for k in range(8):
                        ins = e.matmul(bk[:, 0:ncol], lhsT=hTc[:, k, 128 * t:128 * t + 128], rhs=wq_v[:, k, col0:col0 + ncol],
                                       start=(k == 0), stop=(k == 7))
                    return ins
                S.add("pe", f, reads=[hres2[t], ("wq", 0), ("wq", 1)], writes=[PB(b)])
                return b

            for t in range(4):
                sl = t % 2
                tb = 7
                b = qproj(t, 0, 512)
                headnorm(bank(b), PB(b), 8, gq_a, ("gt", 0), qn[sl], ("qn", sl))
                pst = transposes(qn[sl], ("qn", sl), 4, tb)
                S.add("act", lambda e, pst=pst, t=t: e.activation(
                    out=qaT_v[:, :, 128 * t:128 * t + 128], in_=pst[:, 0:512].rearrange("p (q t) -> p q t", q=4), func=AF.Copy),
                    reads=[PB(tb)], writes=["qaT"])
                b = qproj(t, 512, 512)
                headnorm(bank(b), PB(b), 8, gq_b, ("gt", 2), qn[sl], ("qn", sl))
                pst = transposes(qn[sl], ("qn", sl), 4, tb)
                S.add("act", lambda e, pst=pst, t=t: e.activation(
                    out=qbT_v[0][:, :, 128 * t:128 * t + 128], in_=pst[:, 0:256].rearrange("p (q t) -> p q t", q=2), func=AF.Copy),
                    reads=[PB(tb)], writes=[("qbT", 0)])
                S.add("act", lambda e, pst=pst, t=t: e.activation(
                    out=qbT[1].rearrange("p (q m L) -> p q m L", q=2, m=4)[:, :, :, 32 * t:32 * t + 32],
                    in_=pst[:, 256:512].rearrange("p (q l m) -> p q m l", q=2, m=4), func=AF.Copy),
                    reads=[PB(tb)], writes=[("qbT", 1)])
                b = qproj(t, 1024, 256)
                headnorm(bank(b)[:, 0:256], PB(b), 4, gq_b, ("gt", 2), qn[sl][:, 0:256], ("qn", sl))
                pst = transposes(qn[sl], ("qn", sl), 2, tb)
                for q in range(2):
                    S.add("act", lambda e, pst=pst, t=t, q=q: e.activation(
                        out=qbT[2][:, CH * q:CH * q + CH].rearrange("p (m r L) -> p m r L", m=4, r=4)[:, :, :, 8 * t:8 * t + 8],
                        in_=pst[:, 128 * q:128 * q + 128].rearrange("p (l m r) -> p m r l", m=4, r=4), func=AF.Copy),
                        reads=[PB(tb)], writes=[("qbT", 2)])

            dbg_dump("qaT", qaT, ["qaT"])
            dbg_dump("qbT0", qbT[0], [("qbT", 0)])
            dbg_dump("qbT2", qbT[2], [("qbT", 2)])
            dbg_dump("tabA", tabA, ["tabA"])
            dbg_dump("kaw", kaw, ["kaw"])
            dbg_dump("vae", vae, ["vae"])
            for j in range(4):
                for rr in range(8):
                    r = 8 * c + rr
                    if is_half and ((c == 2 and rr < 4) or (c == 5 and rr >= 5)):
                        srow = rr if c == 2 else rr - 5 + 4
                        wrow = 12 if c == 2 else 40
                        ti = tsctr[0] % 2
                        tsctr[0] += 1
                        S.add("pool", lambda e, srow=srow, j=j, ti=ti: e.dma_start(out=tsb[ti], in_=tS[srow * 4 + j]),
                              writes=[("tsb", ti)], dma=True)
                        si_ = sctr[0] % 2
                        sctr[0] += 1
                        sb_ = 2 * si_
                        Sb = bank(sb_, 2)

                        def fqk6(e, j=j, rr=rr, Sb=Sb, lo=lo, wrow=wrow, ti=ti):
                            for e2 in range(2):
                                e.matmul(Sb[:, e2 * 512:e2 * 512 + 384], lhsT=ident, rhs=tsb[ti][:, e2 * 384:e2 * 384 + 384],
                                         start=True, stop=False)
                            for e2 in range(2):
                                for b6 in range(6):
                                    k0 = (wrow - lo + 2 * b6) * 64
                                    ins = e.matmul(Sb[:, e2 * 512 + b6 * 64:e2 * 512 + b6 * 64 + 64],
                                                   lhsT=kaw_v[64 * e2:64 * e2 + 64, j, k0:k0 + 128],
                                                   rhs=qaT_v[64 * e2:64 * e2 + 64, j, rr * 64:rr * 64 + 64], start=False, stop=(b6 == 5))
                            return ins
                        S.add("pe", fqk6, reads=["kaw", "qaT", ("tsb", ti), "ident"], writes=[PB(sb_), PB(sb_ + 1)])
                        S.add("act", lambda e, si_=si_, Sb=Sb: e.activation(
                            out=ptb[si_][:, 0:768].rearrange("p (h x) -> p h x", h=2),
                            in_=Sb.rearrange("p (h x) -> p h x", h=2)[:, :, 0:384], func=AF.Exp),
                              reads=[PB(sb_), PB(sb_ + 1)], writes=[("ptb", si_)])

                        def fpv6(e, j=j, rr=rr, si_=si_, lo=lo, wrow=wrow):
                            for e2 in range(2):
                                h = 2 * j + e2
                                for b6 in range(6):
                                    vt = vae_v[:, (wrow - lo) // 2 + b6, h * 65:h * 65 + 65]
                                    ins = e.matmul(bank(4 + e2)[0:65, rr * 64:rr * 64 + 64], lhsT=vt,
                                                   rhs=ptb[si_][:, e2 * 384 + b6 * 64:e2 * 384 + b6 * 64 + 64],
                                                   start=(b6 == 0), stop=(b6 == 5))
                            return ins
                        S.add("pe", fpv6, reads=[("ptb", si_), "vae"], writes=[PB(4), PB(5)])
                        continue
                    rs_ = min(max(r - 4, 0), R - 8)
                    blk0 = rs_ - r + 7
                    si_ = sctr[0] % 2
                    sctr[0] += 1
                    sb_ = 2 * si_
                    Sb = bank(sb_, 2)

                    def fqk(e, j=j, rr=rr, rs_=rs_, Sb=Sb, lo=lo, blk0=blk0):
                        for e2 in range(2):
                            for b4 in range(4):
                                e.matmul(Sb[:, e2 * 512 + b4 * 64:e2 * 512 + b4 * 64 + 64], lhsT=ident,
                                         rhs=tabA_v[:, 2 * j + e2, blk0 + 2 * b4, :], start=(b4 == 0), stop=False)
                        for e2 in range(2):
                            for b4 in range(4):
                                k0 = (rs_ - lo + 2 * b4) * 64
                                ins = e.matmul(Sb[:, e2 * 512 + b4 * 64:e2 * 512 + b4 * 64 + 64],
                                               lhsT=kaw_v[64 * e2:64 * e2 + 64, j, k0:k0 + 128],
                                               rhs=qaT_v[64 * e2:64 * e2 + 64, j, rr * 64:rr * 64 + 64], start=False, stop=(b4 == 3))
                        return ins
                    S.add("pe", fqk, reads=["kaw", "qaT", "tabA", "ident"], writes=[PB(sb_), PB(sb_ + 1)])
                    S.add("act", lambda e, si_=si_, Sb=Sb: e.activation(
                        out=ptb[si_][:, 0:512].rearrange("p (h x) -> p h x", h=2),
                        in_=Sb.rearrange("p (h x) -> p h x", h=2)[:, :, 0:256], func=AF.Exp),
                          reads=[PB(sb_), PB(sb_ + 1)], writes=[("ptb", si_)])
                    dbg_dump("sbb", sbb[si_][:, 0:512], [("sbb", si_)], F32)
                    dbg_dump("ptb", ptb[si_][:, 0:512], [("ptb", si_)])
                    odd = (rs_ - lo) % 2

                    def fpv(e, j=j, rr=rr, rs_=rs_, si_=si_, odd=odd, lo=lo):
                        for e2 in range(2):
                            h = 2 * j + e2
                            for b4 in range(4):
                                if odd:
                                    vt = vao_v[:, (rs_ - lo - 1) // 2 + b4, h * 65:h * 65 + 65]
                                else:
                                    vt = vae_v[:, (rs_ - lo) // 2 + b4, h * 65:h * 65 + 65]
                                ins = e.matmul(bank(4 + e2)[0:65, rr * 64:rr * 64 + 64], lhsT=vt,
                                               rhs=ptb[si_][:, e2 * 256 + b4 * 64:e2 * 256 + b4 * 64 + 64],
                                               start=(b4 == 0), stop=(b4 == 3))
                        return ins
                    S.add("pe", fpv, reads=[("ptb", si_), "vae", "vao"], writes=[PB(4), PB(5)])
                for e2 in range(2):
                    h = 2 * j + e2
                    oi = octr[0] % 4
                    octr[0] += 1
                    S.add("act", lambda e, e2=e2, oi=oi: e.activation(out=oas[oi][0:65, :], in_=bank(4 + e2)[0:65, :], func=AF.Copy),
                          reads=[PB(4 + e2)], writes=[("oas", oi)])
                    normalize(oas[oi], ("oas", oi), ost_v[64 * e2:64 * e2 + 64, j, :], "ost")
            t0 = c * CH
            S.add("pool", lambda e, t0=t0: e.dma_start(out=OAT[si][:, :, t0:t0 + CH].rearrange("q p t -> p q t"),
                                                       in_=ost_v),
                  reads=["ost"], writes=[("OAT", si, c)], dma=True)

            for g in range(3):
                dc = (0, 1, 2)[g]
                for m in range(4):
                    if g == 0:
                        n = 4 * c + m
                        kts = [(n + d - t_lo1, d + 1) for d in (-1, 0, 1) if 0 <= n + d < NT]
                        kv = kbw0_v
                        vv = vbw0_v
                        kres, vres = "kbw0", "vbw0"
                    else:
                        ds = [d for d in range(-dc, dc + 1) if 0 <= c + d < NC_]
                        kts = [(i, d + dc) for i, d in enumerate(ds)]
                        wi = m % 2
                        c_lo = c + ds[0]
                        nkt = len(ds)
                        kv = kbwm[g][wi].rearrange("p (q t) -> p q t", q=2)
                        vv = vbwm[g][wi].rearrange("p (n c) -> p n c", c=260)
                        kres, vres = ("kbwm", g, wi), ("vbwm", g, wi)
                        crg = [c + d for d in ds]
                        for q in range(2):
                            S.add("sp", lambda e, g=g, m=m, c_lo=c_lo, nkt=nkt, kv=kv, q=q: e.dma_start(
                                out=kv[:, q, 0:nkt * 128].rearrange("p (n t) -> p n t", n=nkt),
                                in_=KBT[si][g, q].rearrange("p (cc mm t) -> p cc mm t", mm=4, t=128)[:, c_lo:c_lo + nkt, m, :]),
                                reads=[("KBT", si, g, cc) for cc in crg], writes=[kres], dma=True)
                        S.add("sp", lambda e, g=g, m=m, c_lo=c_lo, nkt=nkt, vv=vv: e.dma_start(
                            out=vv[:, 0:nkt, :],
                            in_=VB[si][g].rearrange("(cc mm p) c -> p cc mm c", mm=4, p=128)[:, c_lo:c_lo + nkt, m, :]),
                            reads=[("VB", si, g, cc) for cc in crg], writes=[vres], dma=True)
                    nk = len(kts)
                    k_lo = kts[0][1]
                    ob = 4 + octr[0] % 2
                    octr[0] += 1
                    for h in range(4):
                        j, e2 = h // 2, h % 2
                        si_ = sctr[0] % 2
                        sctr[0] += 1
                        sb_ = 2 * si_
                        Sb = bank(sb_, 2)

                        def fqk(e, g=g, h=h, j=j, e2=e2, m=m, kts=kts, Sb=Sb, kv=kv, nk=nk, k_lo=k_lo):
                            n0 = min(nk, 4)
                            e.matmul(Sb[:, 0:n0 * 128], lhsT=ident,
                                     rhs=tabB[g].rearrange("p (h x) -> p h x", h=4)[:, h, k_lo * 128:(k_lo + n0) * 128],
                                     start=True, stop=False)
                            if nk > 4:
                                e.matmul(Sb[:, 512:640], lhsT=ident,
                                         rhs=tabB[g].rearrange("p (h x) -> p h x", h=4)[:, h, (k_lo + 4) * 128:(k_lo + 5) * 128],
                                         start=True, stop=False)
                            for i, (ws, _) in enumerate(kts):
                                ins = e.matmul(Sb[:, i * 128:i * 128 + 128],
                                               lhsT=kv[64 * e2:64 * e2 + 64, j, ws * 128:ws * 128 + 128],
                                               rhs=qbT_v[g][64 * e2:64 * e2 + 64, j, 128 * m:128 * m + 128],
                                               start=False, stop=(i == min(nk, 4) - 1 or i == nk - 1))
                            return ins
                        S.add("pe", fqk, reads=[kres, ("qbT", g), f"tabB{g}", "ident"], writes=[PB(sb_), PB(sb_ + 1)])
                        S.add("act", lambda e, si_=si_, nk=nk, Sb=Sb: e.activation(out=ptb[si_][:, 0:nk * 128], in_=Sb[:, 0:nk * 128], func=AF.Exp),
                              reads=[PB(sb_), PB(sb_ + 1)], writes=[("ptb", si_)])

                        def fpv(e, h=h, kts=kts, si_=si_, ob=ob, nk=nk, vv=vv):
                            for i, (ws, _) in enumerate(kts):
                                ins = e.matmul(bank(ob)[0:65, 128 * h:128 * h + 128],
                                               lhsT=vv[:, ws, h * 65:h * 65 + 65],
                                               rhs=ptb[si_][:, i * 128:i * 128 + 128], start=(i == 0), stop=(i == nk - 1))
                            return ins
                        S.add("pe", fpv, reads=[("ptb", si_), vres], writes=[PB(ob)])
                    O = bank(ob)[0:65, :].rearrange("p (h q) -> p h q", h=4)
                    tres = [("tot", m)] if g == 0 else [("tot", mm) for mm in range(4)]
                    if g == 0:
                        S.add("act", lambda e, O=O, m=m: e.activation(out=tot_v[0:65, :, 128 * m:128 * m + 128], in_=O, func=AF.Copy),
                              reads=[PB(ob)], writes=tres)
                    elif g == 1:
                        tv = tot_v[0:65, :, :].rearrange("p h (l mm) -> p h mm l", mm=4)[:, :, m, :]
                        S.add("dve", lambda e, O=O, tv=tv: e.tensor_tensor(out=tv, in0=O, in1=tv, op=ALU.add),
                              reads=[PB(ob)] + tres, writes=tres)
                    else:
                        tv = tot_v[0:65, :, :].rearrange("p h (l mm r) -> p h mm r l", mm=4, r=4)[:, :, m]
                        S.add("dve", lambda e, O=O, tv=tv: e.tensor_tensor(
                            out=tv, in0=O.rearrange("p h (r l) -> p h r l", r=4), in1=tv, op=ALU.add),
                            reads=[PB(ob)] + tres, writes=tres)
            for h in range(4):
                normalize(tot_v[0:65, h, :], [("tot", mm) for mm in range(4)], obst_v[64 * (h % 2):64 * (h % 2) + 64, h // 2, :], "obst")
            S.add("pool", lambda e, t0=t0: e.dma_start(out=OBT[si][:, :, t0:t0 + CH].rearrange("q p t -> p q t"),
                                                       in_=obst_v),
                  reads=["obst"], writes=[("OBT", si, c)], dma=True)

    def pass3(si):
        T = T_list[si]
        NC_ = T // CH
        S.filler_bank = None
        A.reset(PERM_MARK)
        NWB_ = 6
        wbuf = [A.alloc(4096, BF16) for _ in range(NWB_)]
        x1 = [A.alloc(4 * D, F32) for _ in range(2)]
        oat = A.alloc(4 * CH, BF16)
        oat_v = oat.rearrange("p (q t) -> p q t", q=4)
        obt = A.alloc(2 * CH, BF16)
        obt_v = obt.rearrange("p (q t) -> p q t", q=2)
        sga = A.alloc(CH, F32)
        sgb = A.alloc(CH, F32)
        tm1 = A.alloc(CH, F32)
        tm2 = A.alloc(CH, F32)
        mT = A.alloc(8 * CH, BF16)
        mT_v = mT.rearrange("p (k t) -> p k t", k=8)
        hmb = [A.alloc(D, BF16) for _ in range(2)]
        hmT = A.alloc(8 * CH, BF16)
        hmT_v = hmT.rearrange("p (k t) -> p k t", k=8)
        sq2 = [A.alloc(256, F32) for _ in range(2)]
        u2 = [A.alloc(256, BF16) for _ in range(3)]
        mst = A.alloc(16, F32)
        gmlp = A.alloc(D, F32)
        S.add("sp", lambda e: e.dma_start(out=gmlp, in_=g_mlp[0].partition_broadcast(128)), writes=["gmlp"], dma=True)
        wctr = [0]
        WINv = WIN.rearrange("(k p) n -> p k n", p=128)
        WOv = WO.rearrange("(k p) n -> p k n", p=128)
        WUPv = WUP.rearrange("(k p) n -> p k n", p=128)
        WAv = WA.rearrange("(q p) n -> p q n", p=128)
        WBv = WB.rearrange("(q p) n -> p q n", p=128)

        def wload(src_ap, wname, **kw):
            i = wctr[0] % NWB_
            wctr[0] += 1
            n = 1
            for s in src_ap.shape[1:]:
                n *= s
            P = src_ap.shape[0]
            dst = wbuf[i][0:P, 0:n]
            if len(src_ap.shape) == 3:
                dst = dst.rearrange("p (a b) -> p a b", a=src_ap.shape[1])
            S.add("sp", lambda e, dst=dst, src_ap=src_ap: e.dma_start(out=dst, in_=src_ap), reads=WRES[wname], writes=[("wbuf", i)], dma=True)
            return dst, ("wbuf", i)

        c_rng = range(2, 6) if si in half else range(NC_)
        for c in c_rng:
            par = c % 2
            X1 = x1[par]
            X1v = X1.rearrange("p (t d) -> p t d", t=4)
            t0 = c * CH
            hTc, hres = Hload(si, c)
            S.add("sp", lambda e, X1v=X1v, t0=t0: e.dma_start(out=X1v, in_=xs[si][t0:t0 + CH, :].rearrange("(t p) d -> p t d", p=128)),
                  writes=[("x1", par)], dma=True)
            S.add("sp", lambda e, t0=t0: e.dma_start(out=oat_v, in_=OAT[si][:, :, t0:t0 + CH].rearrange("q p t -> p q t")),
                  reads=[("OAT", si, c)], writes=["oat"], dma=True)
            S.add("sp", lambda e, t0=t0: e.dma_start(out=obt_v, in_=OBT[si][:, :, t0:t0 + CH].rearrange("q p t -> p q t")),
                  reads=[("OBT", si, c)], writes=["obt"], dma=True)
            for jg in range(2):
                wga, rga = wload(WINv[:, :, 3840 + 512 * jg:3840 + 512 * jg + 512], "WIN")
                wgb, rgb = wload(WINv[:, :, 4864 + 512 * jg:4864 + 512 * jg + 512], "WIN")
                wa, ra = wload(WAv[:, :, 512 * jg:512 * jg + 512], "WA")
                wb, rb = wload(WBv[:, :, 512 * jg:512 * jg + 512], "WB")
                for jj in range(4):
                    j = 4 * jg + jj
                    b0 = 4 * (j % 2)
                    cs = slice(128 * jj, 128 * jj + 128)

                    def fg(e, w, b, cs=cs, hTc=hTc):
                        for k in range(8):
                            ins = e.matmul(bank(b), lhsT=w[:, k, cs], rhs=hTc[:, k, :], start=(k == 0), stop=(k == 7))
                        return ins
                    S.add("pe", lambda e, w=wga, b=b0, fg=fg: fg(e, w, b), reads=hres + [rga], writes=[PB(b0)])
                    S.add("pe", lambda e, w=wgb, b=b0 + 1, fg=fg: fg(e, w, b), reads=hres + [rgb], writes=[PB(b0 + 1)])

                    def fa(e, wa=wa, b=b0 + 2, cs=cs):
                        for h in range(4):
                            ins = e.matmul(bank(b), lhsT=wa[:, h, cs], rhs=oat_v[:, h, :], start=(h == 0), stop=(h == 3))
                        return ins
                    S.add("pe", fa, reads=["oat", ra], writes=[PB(b0 + 2)])

                    def fb(e, wb=wb, b=b0 + 3, cs=cs):
                        for h in range(2):
                            ins = e.matmul(bank(b), lhsT=wb[:, h, cs], rhs=obt_v[:, h, :], start=(h == 0), stop=(h == 1))
                        return ins
                    S.add("pe", fb, reads=["obt", rb], writes=[PB(b0 + 3)])
                    S.add("act", lambda e, b=b0: e.activation(out=sga, in_=bank(b), func=AF.Sigmoid), reads=[PB(b0)], writes=["sga"])
                    S.add("act", lambda e, b=b0 + 1: e.activation(out=sgb, in_=bank(b), func=AF.Sigmoid), reads=[PB(b0 + 1)], writes=["sgb"])
                    S.add("dve", lambda e, b=b0 + 2: e.tensor_tensor(out=tm1, in0=sga, in1=bank(b), op=ALU.mult),
                          reads=["sga", PB(b0 + 2)], writes=["t1"])
                    S.add("dve", lambda e, b=b0 + 3: e.tensor_tensor(out=tm2, in0=sgb, in1=bank(b), op=ALU.mult),
                          reads=["sgb", PB(b0 + 3)], writes=["t2"])
                    S.add("pool", lambda e, j=j: e.tensor_tensor(out=mT_v[:, j, :], in0=tm1, in1=tm2, op=ALU.add),
                          reads=["t1", "t2"], writes=[("mT", j)])
            mres = [("mT", j) for j in range(8)]
            for hf in range(2):
                wo, ro = wload(WOv[:, :, 512 * hf:512 * hf + 512], "WO")
                for t in range(4):
                    b = (t % 2)

                    def fo(e, wo=wo, t=t, b=b):
                        for k in range(8):
                            ins = e.matmul(bank(b), lhsT=mT_v[:, k, 128 * t:128 * t + 128], rhs=wo[:, k, :], start=(k == 0), stop=(k == 7))
                        return ins
                    S.add("pe", fo, reads=mres + [ro], writes=[PB(b)])
                    xs_ = X1v[:, t, 512 * hf:512 * hf + 512]
                    S.add("dve", lambda e, xs_=xs_, b=b: e.tensor_tensor(out=xs_, in0=xs_, in1=bank(b), op=ALU.add),
                          reads=[("x1", par), PB(b)], writes=[("x1", par)])
            for t in range(4):
                s2 = t % 2
                ss = mst[:, t:t + 1]
                sd = mst[:, 4 + t:5 + t]
                rs = mst[:, 8 + t:9 + t]
                xv = X1v[:, t, :]
                S.add("act", lambda e, xv=xv, ss=ss: e.activation(out=junk, in_=xv, func=AF.Square, accum_out=ss),
                      reads=[("x1", par)], writes=["junk", ("mss", t)])
                S.add("act", lambda e, ss=ss, sd=sd: e.activation(out=sd, in_=ss, func=AF.Sqrt, scale=1.0 / D, bias=epst[:, 0:1]),
                      reads=[("mss", t), "eps"], writes=[("msd", t)])
                S.add("dve", lambda e, sd=sd, rs=rs: e.reciprocal(out=rs, in_=sd), reads=[("msd", t)], writes=[("mrs", t)])
                S.add("dve", lambda e, xv=xv, rs=rs, s2=s2: e.scalar_tensor_tensor(
                    out=hmb[s2], in0=xv, scalar=rs, in1=gmlp, op0=ALU.mult, op1=ALU.mult),
                    reads=[("x1", par), ("mrs", t), "gmlp"], writes=[("hmb", s2)])
                tb = 6 + s2
                pst = bankb(tb)

                def f(e, s2=s2, pst=pst):
                    for k in range(8):
                        ins = e.transpose(out=pst[:, 128 * k:128 * k + 128], in_=hmb[s2][:, 128 * k:128 * k + 128], identity=ident)
                    return ins
                S.add("pe", f, reads=[("hmb", s2), "ident"], writes=[PB(tb)])
                S.add("act", lambda e, pst=pst, t=t: e.activation(
                    out=hmT_v[:, :, 128 * t:128 * t + 128], in_=pst.rearrange("p (k t) -> p k t", k=8), func=AF.Copy),
                    reads=[PB(tb)], writes=[("hmT", t)])
            uctr = [0]
            for hh in range(2):
                pend = None
                hm_res = [("hmT", 2 * hh), ("hmT", 2 * hh + 1)]
                for fgp in range(8):
                    wup, rup = wload(WUPv[:, :, 512 * fgp:512 * fgp + 512], "WUP")
                    wdn, rdn = wload(WDN[512 * fgp:512 * fgp + 512, :].rearrange("(k p) n -> p k n", p=128), "WDN")
                    for f4 in range(4):
                        fb_ = 4 * fgp + f4
                        ub = 4 + fb_ % 2
                        ui = uctr[0] % 3
                        si2 = uctr[0] % 2
                        uctr[0] += 1

                        def fup(e, wup=wup, f4=f4, ub=ub, hh=hh):
                            for k in range(8):
                                ins = e.matmul(bank(ub)[:, 0:256], lhsT=wup[:, k, 128 * f4:128 * f4 + 128],
                                               rhs=hmT_v[:, k, 256 * hh:256 * hh + 256], start=(k == 0), stop=(k == 7))
                            return ins
                        S.add("pe", fup, reads=hm_res + [rup], writes=[PB(ub)])
                        S.add("act", lambda e, ub=ub, si2=si2: e.activation(out=sq2[si2], in_=bank(ub)[:, 0:256], func=AF.Square),
                              reads=[PB(ub)], writes=[("sq2", si2)])
                        S.add("dve", lambda e, ub=ub, si2=si2, ui=ui: e.scalar_tensor_tensor(
                            out=u2[ui], in0=bank(ub)[:, 0:256], scalar=0.0, in1=sq2[si2], op0=ALU.is_gt, op1=ALU.mult),
                            reads=[PB(ub), ("sq2", si2)], writes=[("u2", ui)])

                        def fdn(e, wdn=wdn, f4=f4, ui=ui, fb_=fb_):
                            for t2 in range(2):
                                for h2 in range(2):
                                    ins = e.matmul(bank(2 * t2 + h2), lhsT=u2[ui][:, 128 * t2:128 * t2 + 128],
                                                   rhs=wdn[:, f4, 512 * h2:512 * h2 + 512], start=(fb_ == 0), stop=(fb_ == 31))
                            return ins
                        if pend is not None:
                            S.add("pe", pend[0], reads=pend[1], writes=[PB(0), PB(1), PB(2), PB(3)])
                        pend = (fdn, [("u2", ui), rdn])
                S.add("pe", pend[0], reads=pend[1], writes=[PB(0), PB(1), PB(2), PB(3)])
                for t2 in range(2):
                    for h2 in range(2):
                        xs_ = X1v[:, 2 * hh + t2, 512 * h2:512 * h2 + 512]
                        b = 2 * t2 + h2
                        S.add("dve", lambda e, xs_=xs_, b=b: e.tensor_tensor(out=xs_, in0=xs_, in1=bank(b), op=ALU.add),
                              reads=[("x1", par), PB(b)], writes=[("x1", par)])
            y0 = (c - c_rng[0]) * CH
            S.add("pool", lambda e, X1v=X1v, y0=y0: e.dma_start(out=ys[si][y0:y0 + CH, :].rearrange("(t p) d -> p t d", p=128), in_=X1v),
                  reads=[("x1", par)], writes=[("y", si, c)], dma=True)

    step = 0
    for si in range(NS):
        for pf in (pass1, pass2, pass3):
            step += 1
            if step <= upto:
                pf(si)
                S.barrier()

    if max_ops is not None:
        S.ops = S.ops[:max_ops]
    print('n_ops', len(S.ops))
    S.emit(nc, stack)
    stack.close()
    return nc, A.peak


def _special_tables(rpb, hh, R=64):
    out = np.full((7, 4, 128, 2, 6, 64), NEGM, np.float32)
    kc = np.arange(64)[:, None]
    qc = np.arange(64)[None, :]
    cs = np.clip(qc - 8, 0, 48)
    cval = (kc >= cs) & (kc < cs + 16)
    cidx = np.clip(kc - qc + 15, 0, 30)
    for i, lr in enumerate((16, 17, 18, 19, 45, 46, 47)):
        gr = lr - 16 + 32 * hh
        rs = min(max(gr - 4, 0), R - 8)
        wrow = 12 if i < 4 else 40
        for b6 in range(6):
            for kr2 in range(2):
                kgr = wrow + 2 * b6 + kr2 - 16 + 32 * hh
                if not (rs <= kgr < rs + 8):
                    continue
                vals = rpb[:, kgr - gr + 7, :][:, cidx]
                vals = np.where(cval[None], vals, NEGM)
                for j in range(4):
                    for e2 in range(2):
                        out[i, j, kr2 * 64:(kr2 + 1) * 64, e2, b6, :] = vals[2 * j + e2]
    return out.reshape(28, 128, 768)


_CACHE = {}


def _common_inputs(norm_mix, w_in, q_norm_a, k_norm_a, q_norm_b, k_norm_b, rpb_a, t5_bias,
                   w_branch_a, w_branch_b, w_out, norm_mlp, w_up, w_down):
    f = lambda a: np.ascontiguousarray(np.asarray(a, dtype=np.float32))
    ga, ma, gb, mb = _tables(np.asarray(rpb_a, np.float32)[0], np.asarray(t5_bias, np.float32))
    d = {
        "w_in": f(w_in[0]), "w_a": f(w_branch_a[0]), "w_b": f(w_branch_b[0]), "w_o": f(w_out[0]),
        "w_up": f(w_up[0]), "w_dn": f(w_down[0]),
        "g_mix": f(norm_mix[0]).reshape(1, D), "g_mlp": f(norm_mlp[0]).reshape(1, D),
        "g_qa": f(q_norm_a[0]).reshape(1, 64), "g_ka": f(k_norm_a[0]).reshape(1, 64),
        "g_qb": f(q_norm_b[0]).reshape(1, 64), "g_kb": f(k_norm_b[0]).reshape(1, 64),
        "tA_g": ga.reshape(128, -1), "tA_m": ma.reshape(128, -1),
    }
    for g in range(3):
        d[f"tB_g{g}"] = gb[g].reshape(128, -1)
        d[f"tB_m{g}"] = mb[g].reshape(128, -1)
    return d


def kernel(x_prompt, x_sample, norm_mix, w_in, q_norm_a, k_norm_a, q_norm_b, k_norm_b, rpb_a,
           t5_bias, w_branch_a, w_branch_b, w_out, norm_mlp, w_up, w_down):
    x_prompt = np.asarray(x_prompt, np.float32)
    x_sample = np.asarray(x_sample, np.float32)
    Bp, Tp, _ = x_prompt.shape
    Bs, Ts, _ = x_sample.shape
    key = (Ts, Tp)
    assert Tp == 8 * CH and Bp * 2 == 8
    if key not in _CACHE:
        _CACHE[key] = build_program([Ts, Tp], half={1})[0]
    nc = _CACHE[key]
    common = _common_inputs(norm_mix, w_in, q_norm_a, k_norm_a, q_norm_b, k_norm_b, rpb_a, t5_bias,
                            w_branch_a, w_branch_b, w_out, norm_mlp, w_up, w_down)
    rpb = np.asarray(rpb_a, np.float32)[0]
    n = 8
    in_maps = []
    for c in range(n):
        m = dict(common)
        m["x0"] = np.ascontiguousarray(x_sample[c % Bs])
        p, hh = c // 2, c % 2
        xl = np.zeros((8 * CH, D), np.float32)
        s1 = np.zeros((128, 8), np.float32)
        if hh == 0:
            xl[2 * CH:] = x_prompt[p, 0:6 * CH]
            s1[:, 2:] = 1.0
        else:
            xl[:6 * CH] = x_prompt[p, 2 * CH:]
            s1[:, :6] = 1.0
        m["x1"] = xl
        m["slot1"] = s1
        m["tS"] = _special_tables(rpb, hh)
        in_maps.append(m)
    res = run_bass_kernel_spmd(nc, in_maps, core_ids=list(range(n)))
    y_s = np.stack([res.results[c]["y0"] for c in range(Bs)], axis=0).astype(np.float32)
    y_p = np.zeros((Bp, Tp, D), np.float32)
    for c in range(n):
        p, hh = c // 2, c % 2
        y_p[p, hh * 4 * CH:(hh + 1) * 4 * CH] = res.results[c]["y1"]
    return (y_p, y_s)
```

```python
import numpy as np
from contextlib import ExitStack
import concourse.bass as bass
import concourse.mybir as mybir
from concourse.bass_utils import run_bass_kernel_spmd

F32 = mybir.dt.float32
BF16 = mybir.dt.bfloat16
AF = mybir.ActivationFunctionType
ALU = mybir.AluOpType
AX = mybir.AxisListType

D = 1024
NEGM = -30000.0
EPS = 1e-6
CH = 512
GRID_W = 64
DILS = (1, 4, 16)
IN_W = 5888
D_FF = 4096
EPOCH = 12000
FILL = 0.5


class _Op:
    __slots__ = ("eng", "fn", "deps", "dma", "token", "idx", "cost", "fb")


class _CostEng:
    def __init__(self, pool=False):
        self.cost = 0.0
        self.k = 2.0 if pool else 1.0

    def then_inc(self, *a, **k):
        return self

    def matmul(self, out, lhsT=None, rhs=None, **kw):
        n = rhs.free_size()
        self.cost += max(n, 64) / 2.2 * (4.0 if rhs.dtype == F32 else 1.0) + 12
        return self

    def transpose(self, out=None, in_=None, identity=None):
        self.cost += 110
        return self

    def activation(self, out=None, in_=None, func=None, accum_out=None, **kw):
        self.cost += 200 + in_.free_size() * 0.62 + (100 if accum_out is not None else 0)
        return self

    def _ew(self, ap, f=1.06):
        self.cost += (110 + ap.free_size() * f) * self.k
        return self

    def tensor_tensor(self, out=None, in0=None, in1=None, op=None):
        return self._ew(in0, 1.15)

    def tensor_scalar(self, out=None, in0=None, **kw):
        return self._ew(in0)

    def scalar_tensor_tensor(self, out=None, in0=None, **kw):
        return self._ew(in0, 1.2 if in0.free_size() > 600 else 1.06)

    def tensor_copy(self, out=None, in_=None):
        return self._ew(in_, 0.9)

    def tensor_reduce(self, out=None, in_=None, **kw):
        return self._ew(in_)

    def reciprocal(self, out=None, in_=None):
        return self._ew(in_, 8.0)

    def memset(self, ap, c):
        return self._ew(ap)

    def affine_select(self, out=None, in_=None, **kw):
        return self._ew(in_)

    def dma_start(self, out=None, in_=None, **kw):
        self.cost += 1800 + out.nbytes() / 220.0
        return self


class Sched:
    ENGS = ("pe", "act", "dve", "pool", "sp")

    def __init__(self):
        self.ops = []
        self.last_w = {}
        self.readers = {}
        self.barrier_deps = set()
        self.last_on_eng = {}
        self.open_dmas = set()
        self.filler_bank = None
        self.make_filler = None
        self.n_fill = 0

    def add(self, eng, fn, reads=(), writes=(), dma=False):
        deps = set(self.barrier_deps)
        for r in reads:
            if r in self.last_w:
                deps.add(self.last_w[r])
        for w in writes:
            if w in self.last_w:
                deps.add(self.last_w[w])
            deps.update(self.readers.get(w, ()))
        op = _Op()
        op.eng, op.fn, op.deps, op.dma = eng, fn, deps, dma
        op.idx = len(self.ops)
        op.token = None
        ce = _CostEng(pool=(eng == "pool"))
        fn(ce)
        op.cost = ce.cost
        op.fb = self.filler_bank
        self.ops.append(op)
        for r in reads:
            self.readers.setdefault(r, []).append(op.idx)
        for w in writes:
            self.last_w[w] = op.idx
            self.readers[w] = []
        self.last_on_eng[eng] = op.idx
        if dma:
            self.open_dmas.add(op.idx)
        return op.idx

    def barrier(self):
        b = set(self.last_on_eng.values()) | set(self.open_dmas)
        self.barrier_deps = b
        self.open_dmas = set()

    def list_schedule(self):
        import heapq
        ops = self.ops
        n = len(ops)
        succ = [[] for _ in range(n)]
        indeg = [0] * n
        for op in ops:
            indeg[op.idx] = len(op.deps)
            for d in op.deps:
                succ[d].append(op.idx)
        fin = [0.0] * n
        est = [0.0] * n
        LAT = 120.0
        future = {e: [] for e in self.ENGS}
        avail = {e: [] for e in self.ENGS}
        free = {e: 0.0 for e in self.ENGS}
        order = {e: [] for e in self.ENGS}
        dma_free = [0.0]
        prev_pe_end = [0.0]
        for op in ops:
            if indeg[op.idx] == 0:
                heapq.heappush(avail[op.eng], op.idx)
        done = 0
        while done < n:
            best = None
            for e in self.ENGS:
                fu = future[e]
                while fu and fu[0][0] <= free[e]:
                    heapq.heappush(avail[e], heapq.heappop(fu)[1])
                if avail[e]:
                    st = free[e]
                elif fu:
                    st = fu[0][0]
                else:
                    continue
                if best is None or st < best[0]:
                    best = (st, e)
            st, e = best
            if not avail[e]:
                free[e] = st
                fu = future[e]
                while fu and fu[0][0] <= free[e]:
                    heapq.heappush(avail[e], heapq.heappop(fu)[1])
            i = heapq.heappop(avail[e])
            op = ops[i]
            if e == "pe" and FILL and op.fb is not None and self.make_filler is not None:
                gap = st - prev_pe_end[0]
                if 250.0 < gap < 20000.0 and prev_pe_end[0] > 0:
                    nf = min(int((gap - 120.0) * FILL / 56.0), 64)
                    if nf > 0:
                        fop = _Op()
                        fop.eng, fop.deps, fop.dma, fop.token, fop.idx = "pe", set(), False, None, -1
                        fop.fn = self.make_filler(op.fb, nf)
                        fop.cost, fop.fb = nf * 56.0, None
                        order["pe"].append(fop)
                        self.n_fill += nf
            if op.dma:
                free[e] = st + 60.0
                t0 = max(st, dma_free[0])
                xfer = max(op.cost - 1800.0, 0.0)
                dma_free[0] = t0 + xfer
                fin[i] = t0 + xfer + 1800.0
            else:
                free[e] = st + op.cost
                fin[i] = free[e]
                if e == "pe":
                    prev_pe_end[0] = free[e]
            order[e].append(op)
            done += 1
            for j in succ[i]:
                indeg[j] -= 1
                if est[j] < fin[i] + LAT:
                    est[j] = fin[i] + LAT
                if indeg[j] == 0:
                    oj = ops[j]
                    if est[j] <= free[oj.eng]:
                        heapq.heappush(avail[oj.eng], j)
                    else:
                        heapq.heappush(future[oj.eng], (est[j], j))
        self.sim_ns = max(fin) if fin else 0.0
        return order

    def emit(self, nc, stack, reorder=True):
        if reorder:
            per_eng = self.list_schedule()
            print("sim_ns", self.sim_ns, "fillers", self.n_fill)
        else:
            per_eng = {e: [] for e in self.ENGS}
            for op in self.ops:
                per_eng[op.eng].append(op)
        n_eng = {e: 0 for e in self.ENGS}
        for op in self.ops:
            if not op.dma:
                n_eng[op.eng] += 1
        sems = {}
        for e in self.ENGS:
            ne = n_eng[e] // EPOCH + 1
            sems[e] = [stack.enter_context(nc.semaphore(f"s_{e}{i}")) for i in range(ne)]
        NDS = {"sp": 40, "pool": 24, "act": 8}
        dsems = {q: [stack.enter_context(nc.semaphore(f"d_{q}{i}")) for i in range(n)]
                 for q, n in NDS.items()}
        cnt = {e: 0 for e in self.ENGS}
        dcnt = {q: 0 for q in NDS}
        dval = {q: [0] * n for q, n in NDS.items()}
        prev_on_sem = {}
        for ename in self.ENGS:
            for op in per_eng[ename]:
                if op.dma:
                    q = op.eng
                    i = dcnt[q] % NDS[q]
                    dcnt[q] += 1
                    dval[q][i] += 16
                    op.token = (("d", q, i), dval[q][i])
                    prev_on_sem[op.idx] = (("d", q, i), dval[q][i] - 16)
                else:
                    k = cnt[op.eng]
                    cnt[op.eng] += 1
                    op.token = (("e", op.eng, k // EPOCH), k % EPOCH + 1)

        def semh(key):
            return dsems[key[1]][key[2]] if key[0] == "d" else sems[key[1]][key[2]]

        ops = self.ops

        def run(ename, e):
            waited = {}

            def wait(key, val):
                if val <= 0:
                    return
                if waited.get(key, 0) >= val:
                    return
                e.wait_ge(semh(key), val)
                waited[key] = val

            for op in per_eng[ename]:
                need = {}
                for d in op.deps:
                    dop = ops[d]
                    if dop.eng == "pe" and ename == "pe" and not dop.dma:
                        continue
                    key, val = dop.token
                    if need.get(key, 0) < val:
                        need[key] = val
                if op.dma:
                    key, val = prev_on_sem[op.idx]
                    if need.get(key, 0) < val:
                        need[key] = val
                for key, val in need.items():
                    wait(key, val)
                ins = op.fn(e)
                key, val = op.token
                ins.then_inc(semh(key), 16 if op.dma else 1)
            if ename in NDS:
                for i in range(NDS[ename]):
                    wait(("d", ename, i), dval[ename][i])

        block = stack.enter_context(nc.Block())

        @block.tensor
        def _(e):
            run("pe", e)

        @block.scalar
        def _(e):
            run("act", e)

        @block.vector
        def _(e):
            run("dve", e)

        @block.gpsimd
        def _(e):
            run("pool", e)

        @block.sync
        def _(e):
            run("sp", e)


class Arena:
    def __init__(self, ap, ncols):
        self.ap = ap
        self.n = ncols
        self.off = 0
        self.peak = 0

    def alloc(self, cols, dtype=F32):
        w = cols if dtype == F32 else (cols + 1) // 2
        w = (w + 7) // 8 * 8
        a = self.ap[:, self.off:self.off + w]
        self.off += w
        self.peak = max(self.peak, self.off)
        assert self.off <= self.n, f"arena overflow {self.off} > {self.n}"
        if dtype != F32:
            a = a.bitcast(dtype)[:, 0:cols]
        return a

    def mark(self):
        return self.off

    def reset(self, m):
        self.off = m


def _t5_buckets(rel):
    half = 16
    ret = np.where(rel > 0, half, 0)
    n = np.abs(rel)
    max_exact = half // 2
    large = max_exact + (np.log(np.maximum(n, 1) / max_exact)
                         / np.log(1024 / max_exact) * (half - max_exact)).astype(np.int32)
    large = np.minimum(large, half - 1)
    return (ret + np.where(n < max_exact, n, large)).astype(np.int32)


def _tables(rpb, t5):
    kc = np.arange(64)[:, None]
    qc = np.arange(64)[None, :]
    cs = np.clip(qc - 8, 0, 48)
    cval = (kc >= cs) & (kc < cs + 16)
    cidx = np.clip(kc - qc + 15, 0, 30)
    ga = np.zeros((128, 8, 14, 64), np.float32)
    ma = np.zeros((128, 8, 14, 64), np.float32)
    for blk in range(14):
        d = blk - 7
        for kr2 in range(2):
            dr = d + kr2
            vals = rpb[:, dr + 7, :][:, cidx]
            vals = np.where(cval[None], vals, 0.0)
            ga[kr2 * 64:(kr2 + 1) * 64, :, blk, :] = vals.transpose(1, 0, 2)
            ma[kr2 * 64:(kr2 + 1) * 64, :, blk, :] = np.where(cval, 0.0, NEGM)[:, None, :]
    gb, mb = [], []
    pk = np.arange(128)[:, None]
    pq = np.arange(128)[None, :]
    for g in range(3):
        dil = DILS[g]
        nk = 3 if g < 2 else 5
        G = np.zeros((128, 4, nk, 128), np.float32)
        M = np.zeros((128, 4, nk, 128), np.float32)
        for kt in range(nk):
            if g < 2:
                dpos = 128 * (kt - 1) + pk - pq
                valid = np.abs(dpos) <= 64
            else:
                rk, lk = pk // 32, pk % 32
                rq, lq = pq // 32, pq % 32
                dpos = 32 * (kt - 2) + lk - lq
                valid = (rk == rq) & (np.abs(dpos) <= 64)
            bk = _t5_buckets(dpos * dil)
            for h in range(4):
                G[:, h, kt, :] = np.where(valid, t5[bk, 4 * g + h], 0.0)
                M[:, h, kt, :] = np.where(valid, 0.0, NEGM)
        gb.append(G)
        mb.append(M)
    return ga, ma, gb, mb


def build_program(T_list, debug=False, upto=99, max_ops=None, half=None):
    half = set(half or ())
    nc = bass.Bass("TRN2", target_bir_lowering=False)
    NS = len(T_list)

    def din(name, shape, dt=F32):
        return nc.dram_tensor(name, list(shape), dt, kind="ExternalInput").ap()

    def dscr(name, shape, dt=BF16):
        kind = "ExternalOutput" if debug else "Internal"
        return nc.dram_tensor(name, list(shape), dt, kind=kind).ap()

    xs = [din(f"x{i}", (T, D)) for i, T in enumerate(T_list)]
    ys = [nc.dram_tensor(f"y{i}", [(4 * CH if i in half else T), D], F32, kind="ExternalOutput").ap() for i, T in enumerate(T_list)]
    slot1 = din("slot1", (128, 8)) if half else None
    tS = din("tS", (28, 128, 768)) if half else None
    w_in = din("w_in", (D, IN_W))
    w_a = din("w_a", (512, D))
    w_b = din("w_b", (256, D))
    w_o = din("w_o", (D, D))
    w_up = din("w_up", (D, D_FF))
    w_dn = din("w_dn", (D_FF, D))
    g_mix = din("g_mix", (1, D))
    g_mlp = din("g_mlp", (1, D))
    g_qa = din("g_qa", (1, 64))
    g_ka = din("g_ka", (1, 64))
    g_qb = din("g_qb", (1, 64))
    g_kb = din("g_kb", (1, 64))
    tA_g = din("tA_g", (128, 8 * 14 * 64))
    tA_m = din("tA_m", (128, 8 * 14 * 64))
    NKB = (3, 3, 5)
    tB_g = [din(f"tB_g{g}", (128, 4 * NKB[g] * 128)) for g in range(3)]
    tB_m = [din(f"tB_m{g}", (128, 4 * NKB[g] * 128)) for g in range(3)]

    WIN = nc.dram_tensor("WINb", [D, IN_W], BF16, kind="Internal").ap()
    WA = nc.dram_tensor("WAb", [512, D], BF16, kind="Internal").ap()
    WB = nc.dram_tensor("WBb", [256, D], BF16, kind="Internal").ap()
    WO = nc.dram_tensor("WOb", [D, D], BF16, kind="Internal").ap()
    WUP = nc.dram_tensor("WUPb", [D, D_FF], BF16, kind="Internal").ap()
    WDN = nc.dram_tensor("WDNb", [D_FF, D], BF16, kind="Internal").ap()
    KAT = [dscr(f"KAT{i}", (4, 128, T)) for i, T in enumerate(T_list)]
    VA = [dscr(f"VA{i}", (T, 8 * 65)) for i, T in enumerate(T_list)]
    KBT = [dscr(f"KBT{i}", (3, 2, 128, T)) for i, T in enumerate(T_list)]
    VB = [dscr(f"VB{i}", (3, T, 4 * 65)) for i, T in enumerate(T_list)]
    OAT = [dscr(f"OAT{i}", (4, 128, T)) for i, T in enumerate(T_list)]
    OBT = [dscr(f"OBT{i}", (2, 128, T)) for i, T in enumerate(T_list)]
    HTD = [nc.dram_tensor(f"HTD{i}", [128, 8, T], BF16, kind="Internal").ap() for i, T in enumerate(T_list)]
    RDS = nc.dram_tensor("RDS", [8, CH], F32, kind="Internal").ap()

    S = Sched()
    stack = ExitStack()
    dbg_seen = set()

    def dbg_dump(name, ap, res, dt=BF16):
        if not debug or name in dbg_seen:
            return
        dbg_seen.add(name)
        t = nc.dram_tensor("DBG_" + name, [ap.shape[0], ap.shape[1]], dt, kind="ExternalOutput").ap()
        S.add("pool", lambda e: e.dma_start(out=t, in_=ap), reads=res, writes=[("dbg", name)], dma=True)
    ARENA_COLS = 207 * 256
    arena_t = stack.enter_context(nc.sbuf_tensor("arena", [128, ARENA_COLS], F32))
    ps_t = stack.enter_context(nc.psum_tensor("ps", [128, 4096], F32))
    A = Arena(arena_t, ARENA_COLS)

    def bank(b, n=1):
        return ps_t[:, 512 * b:512 * (b + n)]

    def bankb(b):
        return ps_t[:, 512 * b:512 * (b + 1)].bitcast(BF16)

    def PB(b):
        return ("ps", b)

    ident = A.alloc(128, BF16)
    ones_f = A.alloc(64, F32)
    epst = A.alloc(8, F32)
    gmix = A.alloc(D, F32)
    graw = A.alloc(4 * 64, F32)
    gq_a = A.alloc(512, F32)
    gk_a = A.alloc(512, F32)
    gq_b = A.alloc(512, F32)
    gk_b = A.alloc(512, F32)
    junk = A.alloc(D, BF16)
    hbuf = {}
    hT = A.alloc(8 * CH, BF16)
    hTb = A.alloc(8 * CH, BF16)
    stats = A.alloc(64, F32)
    nst = A.alloc(3 * 8 * 2, F32)
    sqb = [A.alloc(512, F32) for _ in range(2)]
    tmpb = [A.alloc(512, F32) for _ in range(2)]
    hT_v = hT.rearrange("p (k t) -> p k t", k=8)
    hT2_v = [hT_v, hTb.rearrange("p (k t) -> p k t", k=8)]

    def Hload(si, c):
        sl = c % 2
        t0 = c * CH
        S.add("sp", lambda e: e.dma_start(out=hT2_v[sl], in_=HTD[si][:, :, t0:t0 + CH]),
              reads=[("HTD", si, c)], writes=[("hT", sl, t) for t in range(4)], dma=True)
        return hT2_v[sl], [("hT", sl, t) for t in range(4)]
    slot1_sb = A.alloc(8, F32)
    tabA = A.alloc(8 * 14 * 64, BF16)
    tabA_v = tabA.rearrange("p (h b q) -> p h b q", h=8, b=14)
    tabB = [A.alloc(4 * NKB[g] * 128, BF16) for g in range(3)]
    tabB_v = [tabB[g].rearrange("p (h k q) -> p h k q", h=4, k=NKB[g]) for g in range(3)]
    tstg = [A.alloc(768, F32) for _ in range(4)]
    PERM_MARK = A.mark()

    def weight_casts(after):
        for (c0, c1, nm) in ((0, 512, "WIN"), (1536, 2304, "WIN"), (3840, 5888, "WIN"), (512, 1536, "WINKV"), (2304, 3840, "WINKV")):
            for r0 in range(0, D, 512):
                S.add("pool", lambda e, c0=c0, c1=c1, r0=r0: e.dma_start(out=WIN[r0:r0 + 512, c0:c1], in_=w_in[r0:r0 + 512, c0:c1]),
                      reads=after, writes=[(nm, c0, r0)], dma=True)
        for nm, src, dst, rows in (("WA", w_a, WA, 512), ("WB", w_b, WB, 256),
                                   ("WO", w_o, WO, 512), ("WUP", w_up, WUP, 256), ("WDN", w_dn, WDN, 1024)):
            R = src.shape[0]
            for r0 in range(0, R, rows):
                S.add("pool", lambda e, s=src, d=dst, r0=r0, rows=rows: e.dma_start(out=d[r0:r0 + rows, :], in_=s[r0:r0 + rows, :]),
                      reads=after, writes=[(nm, r0)], dma=True)

    def setup():
        S.add("pool", lambda e: e.memset(ident, 0.0), writes=["ident"])
        S.add("pool", lambda e: e.affine_select(out=ident, in_=ident, pattern=[[-1, 128]], compare_op=ALU.not_equal,
                                                fill=1.0, base=0, channel_multiplier=1),
              reads=["ident"], writes=["ident"])
        S.add("dve", lambda e: e.memset(ones_f, 1.0), writes=["ones_f"])
        S.add("dve", lambda e: e.memset(epst, EPS), writes=["eps"])
        pass
        S.add("sp", lambda e: e.dma_start(out=gmix, in_=g_mix[0].partition_broadcast(128)), writes=["gmix"], dma=True)
        for i, gsrc in enumerate((g_qa, g_ka, g_qb, g_kb)):
            S.add("sp", lambda e, i=i, gsrc=gsrc: e.dma_start(out=graw[:, 64 * i:64 * i + 64], in_=gsrc[0].partition_broadcast(128)),
                  writes=[("graw", i)], dma=True)
        for i, dst in enumerate((gq_a, gk_a, gq_b, gk_b)):
            def f(e, i=i, dst=dst):
                return e.tensor_copy(out=dst.rearrange("p (h d) -> p h d", h=8),
                                     in_=graw[:, 64 * i:64 * i + 64].unsqueeze(1).to_broadcast([128, 8, 64]))
            S.add("dve", f, reads=[("graw", i)], writes=[("gt", i)])
        if half:
            S.add("sp", lambda e: e.dma_start(out=slot1_sb, in_=slot1), writes=["slot1"], dma=True)
        pc = 0
        for (gsrc, msrc, dst, n, nm) in [(tA_g, tA_m, tabA, 8 * 14 * 64, "tabA")] + \
                [(tB_g[g], tB_m[g], tabB[g], 4 * NKB[g] * 128, f"tabB{g}") for g in range(3)]:
            for p0 in range(0, n, 768):
                w = min(768, n - p0)
                sa, sb2 = tstg[2 * (pc % 2)], tstg[2 * (pc % 2) + 1]
                ra, rb2 = ("tstg", 2 * (pc % 2)), ("tstg", 2 * (pc % 2) + 1)
                pc += 1
                S.add("sp", lambda e, gsrc=gsrc, p0=p0, w=w, sa=sa: e.dma_start(out=sa[:, 0:w], in_=gsrc[:, p0:p0 + w]), writes=[ra], dma=True)
                S.add("sp", lambda e, msrc=msrc, p0=p0, w=w, sb2=sb2: e.dma_start(out=sb2[:, 0:w], in_=msrc[:, p0:p0 + w]), writes=[rb2], dma=True)
                S.add("dve", lambda e, dst=dst, p0=p0, w=w, sa=sa, sb2=sb2: e.tensor_tensor(out=dst[:, p0:p0 + w], in0=sa[:, 0:w], in1=sb2[:, 0:w], op=ALU.add),
                      reads=[ra, rb2], writes=[nm])
        for i, dst in ((0, gq_a), (2, gq_b)):
            S.add("dve", lambda e, dst=dst: e.tensor_scalar(out=dst, in0=dst, scalar1=0.125, scalar2=None, op0=ALU.mult),
                  reads=[("gt", i)], writes=[("gt", i)])

    setup()
    WRES = {"WIN": [("WIN", c0, r0) for c0 in (0, 1536, 3840) for r0 in (0, 512)],
            "WINKV": [("WINKV", c0, r0) for c0 in (512, 2304) for r0 in (0, 512)], "WA": [("WA", 0)], "WB": [("WB", 0)],
            "WO": [("WO", r0) for r0 in range(0, D, 512)], "WUP": [("WUP", r0) for r0 in range(0, D, 256)],
            "WDN": [("WDN", r0) for r0 in range(0, D_FF, 1024)]}

    def make_filler(fb, nf):
        def f(e):
            for _ in range(nf):
                ins = e.matmul(bank(fb)[:, 0:128], lhsT=ident, rhs=ident, start=True, stop=True)
            return ins
        return f
    S.make_filler = make_filler

    tile_ctr = [0]

    def H(x_ap, c, tag):
        for t in range(4):
            i = tile_ctr[0]
            tile_ctr[0] += 1
            s2 = i % 2
            s4 = i % 4
            xtile = hbuf['xt'][s2]
            hb = hbuf['hb']
            r0 = c * CH + t * 128
            S.add("sp", lambda e, xtile=xtile, r0=r0: e.dma_start(out=xtile, in_=x_ap[r0:r0 + 128, :]),
                  writes=[("xt", s2)], dma=True)
            ss = stats[:, s4:s4 + 1]
            sd = stats[:, 4 + s4:5 + s4]
            rs = stats[:, 8 + s4:9 + s4]
            S.add("act", lambda e, xtile=xtile, ss=ss: e.activation(out=junk, in_=xtile, func=AF.Square, accum_out=ss),
                  reads=[("xt", s2)], writes=["junk", ("ss", s4)])
            S.add("act", lambda e, ss=ss, sd=sd: e.activation(out=sd, in_=ss, func=AF.Ln, scale=1.0 / D, bias=epst[:, 0:1]),
                  reads=[("ss", s4), "eps"], writes=[("sd", s4)])
            S.add("act", lambda e, sd=sd, rs=rs: e.activation(out=rs, in_=sd, func=AF.Exp, scale=-0.5), reads=[("sd", s4)], writes=[("rs", s4)])
            S.add("dve", lambda e, xtile=xtile, rs=rs, s2=s2: e.scalar_tensor_tensor(
                out=hb[s2], in0=xtile, scalar=rs, in1=gmix, op0=ALU.mult, op1=ALU.mult),
                reads=[("xt", s2), ("rs", s4), "gmix"], writes=[("hb", s2)])
            tb = 6 + s2
            pst = bankb(tb)

            def f(e, s2=s2, pst=pst):
                for k in range(8):
                    ins = e.transpose(out=pst[:, 128 * k:128 * k + 128], in_=hb[s2][:, 128 * k:128 * k + 128], identity=ident)
                return ins
            S.add("pe", f, reads=[("hb", s2), "ident"], writes=[PB(tb)])
            S.add("act", lambda e, pst=pst, t=t: e.activation(
                out=hT_v[:, :, 128 * t:128 * t + 128], in_=pst.rearrange("p (k t) -> p k t", k=8), func=AF.Copy),
                reads=[PB(tb)], writes=[("hT", 0, t)])

    nslot = [0]

    def headnorm(src, src_res, nh, gain, gain_res, out, out_res):
        sl = nslot[0] % 2
        nslot[0] += 1
        W = nh * 64
        sq = sqb[sl][:, 0:W]
        tmp = tmpb[sl][:, 0:W]
        ssq = nst[:, 24 * sl:24 * sl + nh]
        sd = nst[:, 24 * sl + 8:24 * sl + 8 + nh]
        rs = nst[:, 24 * sl + 16:24 * sl + 16 + nh]
        S.add("act", lambda e: e.activation(out=sq, in_=src, func=AF.Square), reads=[src_res], writes=[("sq", sl)])
        S.add("dve", lambda e: e.tensor_reduce(out=ssq, in_=sq.rearrange("p (h d) -> p h d", h=nh), axis=AX.X, op=ALU.add),
              reads=[("sq", sl)], writes=[("nssq", sl)])
        S.add("act", lambda e: e.activation(out=sd, in_=ssq, func=AF.Ln, scale=1.0 / 64, bias=epst[:, 0:1]),
              reads=[("nssq", sl), "eps"], writes=[("nsd", sl)])
        S.add("act", lambda e: e.activation(out=rs, in_=sd, func=AF.Exp, scale=-0.5), reads=[("nsd", sl)], writes=[("nrs", sl)])
        S.add("dve", lambda e: e.tensor_tensor(out=tmp.rearrange("p (h d) -> p h d", h=nh),
                                               in0=src.rearrange("p (h d) -> p h d", h=nh),
                                               in1=rs.unsqueeze(2).to_broadcast([128, nh, 64]), op=ALU.mult),
              reads=[src_res, ("nrs", sl)], writes=[("tmp", sl)])
        S.add("dve", lambda e: e.tensor_tensor(out=out, in0=tmp, in1=gain[:, 0:W], op=ALU.mult),
              reads=[("tmp", sl), gain_res], writes=[out_res])

    def transposes(src, src_res, n, tb):
        pst = bankb(tb)

        def f(e):
            for k in range(n):
                ins = e.transpose(out=pst[:, 128 * k:128 * k + 128], in_=src[:, 128 * k:128 * k + 128], identity=ident)
            return ins
        S.add("pe", f, reads=[src_res, "ident"], writes=[PB(tb)])
        return pst

    def pass1(si):
        T = T_list[si]
        NC_ = T // CH
        S.filler_bank = 0
        A.reset(PERM_MARK)
        hbuf['xt'] = [A.alloc(D, F32) for _ in range(2)]
        hbuf['hb'] = [A.alloc(D, BF16) for _ in range(2)]
        wkv = A.alloc(8 * 2560, BF16)
        wkv_v = wkv.rearrange("p (k n) -> p k n", k=8)
        hT3 = A.alloc(8 * CH, BF16)
        hT3_v = hT3.rearrange("p (k t) -> p k t", k=8)
        kn = [A.alloc(512, BF16) for _ in range(2)]
        kst = [A.alloc(4 * CH, BF16) for _ in range(2)]
        vst = [A.alloc(4 * 8 * 65, BF16) for _ in range(2)]
        kbst = [[A.alloc(2 * CH, BF16) for _ in range(3)] for _ in range(2)]
        vbst = [[A.alloc(4 * 4 * 65, BF16) for _ in range(3)] for _ in range(2)]
        segs = [(512, 512), (1024, 512), (2304, 256), (3072, 256), (2560, 256), (2816, 256), (3328, 256), (3584, 256)]
        off = 0
        WINv = WIN.rearrange("(k p) n -> p k n", p=128)
        if si == 0:
            w_in_v = w_in.rearrange("(k p) n -> p k n", p=128)
            wstg = [A.alloc(8 * 256, F32) for _ in range(2)]
            pi = 0
            for (c0, w) in segs:
                for q0 in range(0, w, 256):
                    st = wstg[pi % 2].rearrange("p (k n) -> p k n", k=8)
                    rs_ = ("wstg", pi % 2)
                    S.add("sp", lambda e, st=st, c0=c0, q0=q0: e.dma_start(out=st, in_=w_in_v[:, :, c0 + q0:c0 + q0 + 256]),
                          writes=[rs_], dma=True)
                    dstv = wkv_v[:, :, off + q0:off + q0 + 256]
                    if pi % 2 == 0:
                        S.add("dve", lambda e, st=st, dstv=dstv: e.tensor_copy(out=dstv, in_=st), reads=[rs_], writes=[("wkv", off)])
                    else:
                        S.add("act", lambda e, st=st, dstv=dstv: e.activation(out=dstv, in_=st, func=AF.Copy), reads=[rs_], writes=[("wkv", off)])
                    pi += 1
                off += w
            weight_casts([("wstg", 0), ("wstg", 1)])
        else:
            for (c0, w) in segs:
                S.add("sp", lambda e, off=off, c0=c0, w=w: e.dma_start(out=wkv_v[:, :, off:off + w], in_=WINv[:, :, c0:c0 + w]),
                      reads=WRES["WINKV"], writes=[("wkv", off)], dma=True)
                off += w
        wkv_res = [("wkv", o) for o in (0, 512, 1024, 1280, 1536, 1792, 2048, 2304)]
        for p in range(2):
            S.add("dve", lambda e, p=p: e.memset(vst[p], 1.0), writes=[("vst", p)])
            for g in range(3):
                S.add("pool", lambda e, p=p, g=g: e.memset(vbst[p][g], 1.0), writes=[("vbst", p, g)])
        gctr = [0]

        def proj(lhs_fn, lhs_res, col0, ncol, bcol0=0, b=None, wres=()):
            if b is None:
                b = 4 + gctr[0] % 2
                gctr[0] += 1
            bk = bank(b)

            def f(e):
                for k in range(8):
                    ins = e.matmul(bk[:, bcol0:bcol0 + ncol], lhsT=lhs_fn(k), rhs=wkv_v[:, k, col0:col0 + ncol],
                                   start=(k == 0), stop=(k == 7))
                return ins
            S.add("pe", f, reads=list(lhs_res) + list(wres), writes=[PB(b)])
            return b

        for c in range(NC_):
            par = c % 2
            if si in half:
                S.add("dve", lambda e, par=par, c=c: e.tensor_copy(
                    out=vst[par].rearrange("p (n e) -> p n e", e=65)[:, :, 64:65],
                    in_=slot1_sb[:, c:c + 1].unsqueeze(1).to_broadcast([128, 32, 1])),
                    reads=["slot1"], writes=[("vst", par)])
                for g in range(3):
                    S.add("dve", lambda e, par=par, c=c, g=g: e.tensor_copy(
                        out=vbst[par][g].rearrange("p (n e) -> p n e", e=65)[:, :, 64:65],
                        in_=slot1_sb[:, c:c + 1].unsqueeze(1).to_broadcast([128, 16, 1])),
                        reads=["slot1"], writes=[("vbst", par, g)])
            H(xs[si], c, "p1")
            hres = [("hT", 0, t) for t in range(4)]
            S.add("sp", lambda e, c=c: e.dma_start(out=HTD[si][:, :, c * CH:c * CH + CH], in_=hT_v),
                  reads=hres, writes=[("HTD", si, c)], dma=True)
            for m in range(4):
                S.add("pool", lambda e, m=m: e.tensor_copy(
                    out=hT3_v[:, :, 128 * m:128 * m + 128].rearrange("p k (r l) -> p k r l", r=4),
                    in_=hT_v.rearrange("p k (l m r) -> p k m r l", m=4, r=4)[:, :, m]),
                    reads=hres, writes=[("hT3", m)])
            for t in range(4):
                lf = lambda k, t=t: hT_v[:, k, 128 * t:128 * t + 128]
                b = proj(lf, [("hT", 0, t)], 0, 512, wres=[wkv_res[0]])
                sl = t % 2
                headnorm(bank(b), PB(b), 8, gk_a, ("gt", 1), kn[sl], ("kn", sl))
                tb = 6 + t % 2
                pst = transposes(kn[sl], ("kn", sl), 4, tb)
                S.add("act", lambda e, pst=pst, t=t, par=par: e.activation(
                    out=kst[par].rearrange("p (q t) -> p q t", q=4)[:, :, 128 * t:128 * t + 128],
                    in_=pst[:, 0:512].rearrange("p (q t) -> p q t", q=4), func=AF.Copy),
                    reads=[PB(tb)], writes=[("kst", par)])
                b = proj(lf, [("hT", 0, t)], 512, 512, wres=[wkv_res[1]])
                S.add("act", lambda e, b=b, t=t, par=par: e.activation(
                    out=vst[par].rearrange("p (t h e) -> p t h e", t=4, h=8)[:, t, :, 0:64],
                    in_=bank(b).rearrange("p (h e) -> p h e", h=8), func=AF.Copy),
                    reads=[PB(b)], writes=[("vst", par)])
                b = proj(lf, [("hT", 0, t)], 1024, 512, wres=[wkv_res[2], wkv_res[3]])
                sl = t % 2
                headnorm(bank(b)[:, 0:256], PB(b), 4, gk_b, ("gt", 3), kn[sl][:, 0:256], ("kn", sl))
                pst = transposes(kn[sl], ("kn", sl), 2, tb)
                S.add("act", lambda e, pst=pst, t=t, par=par: e.activation(
                    out=kbst[par][0].rearrange("p (q t) -> p q t", q=2)[:, :, 128 * t:128 * t + 128],
                    in_=pst[:, 0:256].rearrange("p (q t) -> p q t", q=2), func=AF.Copy),
                    reads=[PB(tb)], writes=[("kbst", par, 0)])
                S.add("act", lambda e, b=b, t=t, par=par: e.activation(
                    out=vbst[par][0].rearrange("p (t h e) -> p t h e", t=4, h=4)[:, t, :, 0:64],
                    in_=bank(b)[:, 256:512].rearrange("p (h e) -> p h e", h=4), func=AF.Copy),
                    reads=[PB(b)], writes=[("vbst", par, 0)])
                b = proj(lf, [("hT", 0, t)], 1536, 512, wres=[wkv_res[4], wkv_res[5]])
                headnorm(bank(b), PB(b), 8, gk_b, ("gt", 3), kn[sl], ("kn", sl))
                pst = transposes(kn[sl], ("kn", sl), 4, tb)
                S.add("act", lambda e, pst=pst, t=t, par=par: e.activation(
                    out=kbst[par][1].rearrange("p (q m L) -> p q m L", q=2, m=4)[:, :, :, 32 * t:32 * t + 32],
                    in_=pst[:, 0:256].rearrange("p (q l m) -> p q m l", q=2, m=4), func=AF.Copy),
                    reads=[PB(tb)], writes=[("kbst", par, 1)])
                for q in range(2):
                    S.add("act", lambda e, pst=pst, t=t, par=par, q=q: e.activation(
                        out=kbst[par][2][:, CH * q:CH * q + CH].rearrange("p (m r L) -> p m r L", m=4, r=4)[:, :, :, 8 * t:8 * t + 8],
                        in_=pst[:, 256 + 128 * q:384 + 128 * q].rearrange("p (l m r) -> p m r l", m=4, r=4), func=AF.Copy),
                        reads=[PB(tb)], writes=[("kbst", par, 2)])
            for m in range(4):
                b = 4 + gctr[0] % 2
                gctr[0] += 1
                proj(lambda k, m=m: hT_v[:, k, :].rearrange("p (l m) -> p m l", m=4)[:, m, :], hres, 2048, 256, 0, b,
                     wres=[wkv_res[6]])
                proj(lambda k, m=m: hT3_v[:, k, 128 * m:128 * m + 128], [("hT3", m)], 2304, 256, 256, b, wres=[wkv_res[7]])
                for g in (1, 2):
                    S.add("act", lambda e, b=b, m=m, par=par, g=g: e.activation(
                        out=vbst[par][g].rearrange("p (t h e) -> p t h e", t=4, h=4)[:, m, :, 0:64],
                        in_=bank(b)[:, 256 * (g - 1):256 * g].rearrange("p (h e) -> p h e", h=4), func=AF.Copy),
                        reads=[PB(b)], writes=[("vbst", par, g)])
            t0 = c * CH
            S.add("sp", lambda e, par=par, t0=t0: e.dma_start(
                out=KAT[si][:, :, t0:t0 + CH].rearrange("q p t -> p q t"),
                in_=kst[par].rearrange("p (q t) -> p q t", q=4)),
                reads=[("kst", par)], writes=[("KAT", si, c)], dma=True)
            S.add("sp", lambda e, par=par, t0=t0: e.dma_start(
                out=VA[si][t0:t0 + CH, :].rearrange("(t p) n -> p t n", p=128),
                in_=vst[par].rearrange("p (t n) -> p t n", t=4)),
                reads=[("vst", par)], writes=[("VA", si, c)], dma=True)
            for g in range(3):
                S.add("sp", lambda e, par=par, t0=t0, g=g: e.dma_start(
                    out=KBT[si][g, :, :, t0:t0 + CH].rearrange("q p t -> p q t"),
                    in_=kbst[par][g].rearrange("p (q t) -> p q t", q=2)),
                    reads=[("kbst", par, g)], writes=[("KBT", si, g, c)], dma=True)
                S.add("sp", lambda e, par=par, t0=t0, g=g: e.dma_start(
                    out=VB[si][g, t0:t0 + CH, :].rearrange("(t p) n -> p t n", p=128),
                    in_=vbst[par][g].rearrange("p (t n) -> p t n", t=4)),
                    reads=[("vbst", par, g)], writes=[("VB", si, g, c)], dma=True)

    def pass2(si):
        T = T_list[si]
        NC_ = T // CH
        R = T // GRID_W
        NT = T // 128
        S.filler_bank = 6
        A.reset(PERM_MARK)
        wq = A.alloc(8 * 1280, BF16)
        wq_v = wq.rearrange("p (k n) -> p k n", k=8)
        qn = [A.alloc(512, BF16) for _ in range(2)]
        qaT = A.alloc(4 * CH, BF16)
        qaT_v = qaT.rearrange("p (q t) -> p q t", q=4)
        qbT = [A.alloc(2 * CH, BF16) for _ in range(3)]
        qbT_v = [qbT[g].rearrange("p (q t) -> p q t", q=2) for g in range(3)]
        kaw = A.alloc(4 * 16 * 64, BF16)
        kaw_v = kaw.rearrange("p (q t) -> p q t", q=4)
        vae = A.alloc(8 * 520, BF16)
        vao = A.alloc(7 * 520, BF16)
        vae_v = vae.rearrange("p (n c) -> p n c", c=520)
        vao_v = vao.rearrange("p (n c) -> p n c", c=520)
        kbw0 = A.alloc(2 * 6 * 128, BF16)
        kbw0_v = kbw0.rearrange("p (q t) -> p q t", q=2)
        vbw0 = A.alloc(6 * 260, BF16)
        vbw0_v = vbw0.rearrange("p (n c) -> p n c", c=260)
        kbwm = [None] + [[A.alloc(2 * NKB[g] * 128, BF16) for _ in range(2)] for g in (1, 2)]
        vbwm = [None] + [[A.alloc(NKB[g] * 260, BF16) for _ in range(2)] for g in (1, 2)]
        sbb = [A.alloc(768, F32) for _ in range(2)]
        ptb = [A.alloc(768, BF16) for _ in range(2)]
        tot = A.alloc(4 * CH, F32)
        tot_v = tot.rearrange("p (h t) -> p h t", h=4)
        oas = [A.alloc(CH, F32) for _ in range(4)]
        lnd = A.alloc(CH, F32)
        rdb = [A.alloc(CH, F32) for _ in range(4)]
        bcs = [A.alloc(CH, F32) for _ in range(4)]
        ost = A.alloc(4 * CH, BF16)
        ost_v = ost.rearrange("p (q t) -> p q t", q=4)
        obst = A.alloc(2 * CH, BF16)
        obst_v = obst.rearrange("p (q t) -> p q t", q=2)

        WINv = WIN.rearrange("(k p) n -> p k n", p=128)
        S.add("sp", lambda e: e.dma_start(out=wq_v[:, :, 0:512], in_=WINv[:, :, 0:512]), reads=WRES["WIN"], writes=[("wq", 0)], dma=True)
        S.add("sp", lambda e: e.dma_start(out=wq_v[:, :, 512:1280], in_=WINv[:, :, 1536:2304]), reads=WRES["WIN"], writes=[("wq", 1)], dma=True)
        sctr = [0]
        octr = [0]

        rctr = [0]

        def normalize(src_ap, src_res, dst_ap, dst_res):
            src_rl = src_res if isinstance(src_res, list) else [src_res]
            ri = rctr[0] % 4
            rctr[0] += 1
            S.add("act", lambda e: e.activation(out=lnd[64:65, :], in_=src_ap[64:65, :], func=AF.Ln),
                  reads=src_rl, writes=["lnd"])
            S.add("act", lambda e: e.activation(out=rdb[ri][64:65, :], in_=lnd[64:65, :], func=AF.Exp, scale=-1.0),
                  reads=["lnd"], writes=[("rd", ri)])
            S.add("sp", lambda e: e.dma_start(out=RDS[ri:ri + 1, :], in_=rdb[ri][64:65, :]),
                  reads=[("rd", ri)], writes=[("RDS", ri)], dma=True)
            S.add("sp", lambda e: e.dma_start(out=bcs[ri][0:64, :], in_=RDS[ri].partition_broadcast(64)),
                  reads=[("RDS", ri)], writes=[("bcs", ri)], dma=True)
            S.add("dve", lambda e: e.tensor_tensor(out=dst_ap, in0=src_ap[0:64, :], in1=bcs[ri][0:64, :], op=ALU.mult),
                  reads=src_rl + [("bcs", ri)], writes=[dst_res])

        gctr = [0]
        is_half = si in half
        c_rng = range(2, 6) if is_half else range(NC_)
        tsb = [A.alloc(768, BF16) for _ in range(2)] if is_half else None
        tsctr = [0]
        for c in c_rng:
            hTc, hres2 = Hload(si, c)
            lo = max(8 * c - 4, 0)
            hi = min(8 * c + (12 if is_half else 11), R)
            nrow = hi - lo
            S.add("sp", lambda e, lo=lo, nrow=nrow: e.dma_start(
                out=kaw_v[:, :, 0:nrow * 64], in_=KAT[si][:, :, lo * 64:(lo + nrow) * 64].rearrange("q p t -> p q t")),
                reads=[("KAT", si, cc) for cc in range(max(c - 1, 0), min(c + 2, NC_))], writes=["kaw"], dma=True)
            n_e = (nrow + 1) // 2
            if lo + 2 * n_e > R:
                n_e -= 1
            n_o = (hi - (lo + 1)) // 2
            S.add("sp", lambda e, lo=lo, n_e=n_e: e.dma_start(
                out=vae_v[:, 0:n_e, :], in_=VA[si][lo * 64:lo * 64 + n_e * 128, :].rearrange("(n p) c -> p n c", p=128)),
                reads=[("VA", si, cc) for cc in range(max(c - 1, 0), min(c + 2, NC_))], writes=["vae"], dma=True)
            S.add("sp", lambda e, lo=lo, n_o=n_o: e.dma_start(
                out=vao_v[:, 0:n_o, :], in_=VA[si][lo * 64 + 64:lo * 64 + 64 + n_o * 128, :].rearrange("(n p) c -> p n c", p=128)),
                reads=[("VA", si, cc) for cc in range(max(c - 1, 0), min(c + 2, NC_))], writes=["vao"], dma=True)
            t_lo1, t_hi1 = max(4 * c - 1, 0), min(4 * c + 5, NT)
            nt1 = t_hi1 - t_lo1
            cr1 = list(range(max(c - 1, 0), min(c + 2, NC_)))
            S.add("sp", lambda e, t_lo1=t_lo1, nt1=nt1: e.dma_start(
                out=kbw0_v[:, :, 0:nt1 * 128], in_=KBT[si][0, :, :, t_lo1 * 128:(t_lo1 + nt1) * 128].rearrange("q p t -> p q t")),
                reads=[("KBT", si, 0, cc) for cc in cr1], writes=["kbw0"], dma=True)
            S.add("sp", lambda e, t_lo1=t_lo1, nt1=nt1: e.dma_start(
                out=vbw0_v[:, 0:nt1, :], in_=VB[si][0, t_lo1 * 128:(t_lo1 + nt1) * 128, :].rearrange("(n p) c -> p n c", p=128)),
                reads=[("VB", si, 0, cc) for cc in cr1], writes=["vbw0"], dma=True)

            def qproj(t, col0, ncol, hTc=hTc, hres2=hres2):
                b = 4 + gctr[0] % 2
                gctr[0] += 1
                bk = bank(b)

                def f(e):
                    for k in range(8):
                        ins = e.matmul(bk[:, 0:ncol], lhsT=hTc[:, k, 128 * t:128 * t + 128], rhs=wq_v[:, k, col0:col0 + ncol],
                                       start=(k == 0), stop=(k == 7))
                    return ins
                S.add("pe", f, reads=[hres2[t], ("wq", 0), ("wq", 1)], writes=[PB(b)])
                return b

            for t in range(4):
                sl = t % 2
                tb = 7
                b = qproj(t, 0, 512)
                headnorm(bank(b), PB(b), 8, gq_a, ("gt", 0), qn[sl], ("qn", sl))
                pst = transposes(qn[sl], ("qn", sl), 4, tb)
                S.add("act", lambda e, pst=pst, t=t: e.activation(
                    out=qaT_v[:, :, 128 * t:128 * t + 128], in_=pst[:, 0:512].rearrange("p (q t) -> p q t", q=4), func=AF.Copy),
                    reads=[PB(tb)], writes=["qaT"])
                b = qproj(t, 512, 512)
                headnorm(bank(b), PB(b), 8, gq_b, ("gt", 2), qn[sl], ("qn", sl))
                pst = transposes(qn[sl], ("qn", sl), 4, tb)
                S.add("act", lambda e, pst=pst, t=t: e.activation(
                    out=qbT_v[0][:, :, 128 * t:128 * t + 128], in_=pst[:, 0:256].rearrange("p (q t) -> p q t", q=2), func=AF.Copy),
                    reads=[PB(tb)], writes=[("qbT", 0)])
                S.add("act", lambda e, pst=pst, t=t: e.activation(
                    out=qbT[1].rearrange("p (q m L) -> p q m L", q=2, m=4)[:, :, :, 32 * t:32 * t + 32],
                    in_=pst[:, 256:512].rearrange("p (q l m) -> p q m l", q=2, m=4), func=AF.Copy),
                    reads=[PB(tb)], writes=[("qbT", 1)])
                b = qproj(t, 1024, 256)
                headnorm(bank(b)[:, 0:256], PB(b), 4, gq_b, ("gt", 2), qn[sl][:, 0:256], ("qn", sl))
                pst = transposes(qn[sl], ("qn", sl), 2, tb)
                for q in range(2):
                    S.add("act", lambda e, pst=pst, t=t, q=q: e.activation(
                        out=qbT[2][:, CH * q:CH * q + CH].rearrange("p (m r L) -> p m r L", m=4, r=4)[:, :, :, 8 * t:8 * t + 8],
                        in_=pst[:, 128 * q:128 * q + 128].rearrange("p (l m r) -> p m r l", m=4, r=4), func=AF.Copy),
                        reads=[PB(tb)], writes=[("qbT", 2)])

            dbg_dump("qaT", qaT, ["qaT"])
            dbg_dump("qbT0", qbT[0], [("qbT", 0)])
            dbg_dump("qbT2", qbT[2], [("qbT", 2)])
            dbg_dump("tabA", tabA, ["tabA"])
            dbg_dump("kaw", kaw, ["kaw"])
            dbg_dump("vae", vae, ["vae"])
            for j in range(4):
                for rr in range(8):
                    r = 8 * c + rr
                    if is_half and ((c == 2 and rr < 4) or (c == 5 and rr >= 5)):
                        srow = rr if c == 2 else rr - 5 + 4
                        wrow = 12 if c == 2 else 40
                        ti = tsctr[0] % 2
                        tsctr[0] += 1
                        S.add("pool", lambda e, srow=srow, j=j, ti=ti: e.dma_start(out=tsb[ti], in_=tS[srow * 4 + j]),
                              writes=[("tsb", ti)], dma=True)
                        si_ = sctr[0] % 2
                        sctr[0] += 1
                        sb_ = 2 * si_
                        Sb = bank(sb_, 2)

                        def fqk6(e, j=j, rr=rr, Sb=Sb, lo=lo, wrow=wrow, ti=ti):
                            for e2 in range(2):
                                e.matmul(Sb[:, e2 * 512:e2 * 512 + 384], lhsT=ident, rhs=tsb[ti][:, e2 * 384:e2 * 384 + 384],
                                         start=True, stop=False)
                            for e2 in range(2):
                                for b6 in range(6):
                                    k0 = (wrow - lo + 2 * b6) * 64
                                    ins = e.matmul(Sb[:, e2 * 512 + b6 * 64:e2 * 512 + b6 * 64 + 64],
                                                   lhsT=kaw_v[64 * e2:64 * e2 + 64, j, k0:k0 + 128],
                                                   rhs=qaT_v[64 * e2:64 * e2 + 64, j, rr * 64:rr * 64 + 64], start=False, stop=(b6 == 5))
                            return ins
                        S.add("pe", fqk6, reads=["kaw", "qaT", ("tsb", ti), "ident"], writes=[PB(sb_), PB(sb_ + 1)])
                        S.add("act", lambda e, si_=si_, Sb=Sb: e.activation(
                            out=ptb[si_][:, 0:768].rearrange("p (h x) -> p h x", h=2),
                            in_=Sb.rearrange("p (h x) -> p h x", h=2)[:, :, 0:384], func=AF.Exp),
                              reads=[PB(sb_), PB(sb_ + 1)], writes=[("ptb", si_)])

                        def fpv6(e, j=j, rr=rr, si_=si_, lo=lo, wrow=wrow):
                            for e2 in range(2):
                                h = 2 * j + e2
                                for b6 in range(6):
                                    vt = vae_v[:, (wrow - lo) // 2 + b6, h * 65:h * 65 + 65]
                                    ins = e.matmul(bank(4 + e2)[0:65, rr * 64:rr * 64 + 64], lhsT=vt,
                                                   rhs=ptb[si_][:, e2 * 384 + b6 * 64:e2 * 384 + b6 * 64 + 64],
                                                   start=(b6 == 0), stop=(b6 == 5))
                            return ins
                        S.add("pe", fpv6, reads=[("ptb", si_), "vae"], writes=[PB(4), PB(5)])
                        continue
                    rs_ = min(max(r - 4, 0), R - 8)
                    blk0 = rs_ - r + 7
                    si_ = sctr[0] % 2
                    sctr[0] += 1
                    sb_ = 2 * si_
                    Sb = bank(sb_, 2)

                    def fqk(e, j=j, rr=rr, rs_=rs_, Sb=Sb, lo=lo, blk0=blk0):
                        for e2 in range(2):
                            for b4 in range(4):
                                e.matmul(Sb[:, e2 * 512 + b4 * 64:e2 * 512 + b4 * 64 + 64], lhsT=ident,
                                         rhs=tabA_v[:, 2 * j + e2, blk0 + 2 * b4, :], start=(b4 == 0), stop=False)
                        for e2 in range(2):
                            for b4 in range(4):
                                k0 = (rs_ - lo + 2 * b4) * 64
                                ins = e.matmul(Sb[:, e2 * 512 + b4 * 64:e2 * 512 + b4 * 64 + 64],
                                               lhsT=kaw_v[64 * e2:64 * e2 + 64, j, k0:k0 + 128],
                                               rhs=qaT_v[64 * e2:64 * e2 + 64, j, rr * 64:rr * 64 + 64], start=False, stop=(b4 == 3))
                        return ins
                    S.add("pe", fqk, reads=["kaw", "qaT", "tabA", "ident"], writes=[PB(sb_), PB(sb_ + 1)])
                    S.add("act", lambda e, si_=si_, Sb=Sb: e.activation(
                        out=ptb[si_][:, 0:512].rearrange("p (h x) -> p h x", h=2),
                        in_=Sb.rearrange("p (h x) -> p h x", h=2)[:, :, 0:256], func=AF.Exp),
                          reads=[PB(sb_), PB(sb_ + 1)], writes=[("ptb", si_)])
                    dbg_dump("sbb", sbb[si_][:, 0:512], [("sbb", si_)], F32)
                    dbg_dump("ptb", ptb[si_][:, 0:512], [("ptb", si_)])
                    odd = (rs_ - lo) % 2

                    def fpv(e, j=j, rr=rr, rs_=rs_, si_=si_, odd=odd, lo=lo):
                        for e2 in range(2):
                            h = 2 * j + e2
                            for b4 in range(4):
                                if odd:
                                    vt = vao_v[:, (rs_ - lo - 1) // 2 + b4, h * 65:h * 65 + 65]
                                else:
                                    vt = vae_v[:, (rs_ - lo) // 2 + b4, h * 65:h * 65 + 65]
                                ins = e.matmul(bank(4 + e2)[0:65, rr * 64:rr * 64 + 64], lhsT=vt,
                                               rhs=ptb[si_][:, e2 * 256 + b4 * 64:e2 * 256 + b4 * 64 + 64],
                                               start=(b4 == 0), stop=(b4 == 3))
                        return ins
                    S.add("pe", fpv, reads=[("ptb", si_), "vae", "vao"], writes=[PB(4), PB(5)])
                for e2 in range(2):
                    h = 2 * j + e2
                    oi = octr[0] % 4
                    octr[0] += 1
                    S.add("act", lambda e, e2=e2, oi=oi: e.activation(out=oas[oi][0:65, :], in_=bank(4 + e2)[0:65, :], func=AF.Copy),
                          reads=[PB(4 + e2)], writes=[("oas", oi)])
                    normalize(oas[oi], ("oas", oi), ost_v[64 * e2:64 * e2 + 64, j, :], "ost")
            t0 = c * CH
            S.add("pool", lambda e, t0=t0: e.dma_start(out=OAT[si][:, :, t0:t0 + CH].rearrange("q p t -> p q t"),
                                                       in_=ost_v),
                  reads=["ost"], writes=[("OAT", si, c)], dma=True)

            for g in range(3):
                dc = (0, 1, 2)[g]
                for m in range(4):
                    if g == 0:
                        n = 4 * c + m
                        kts = [(n + d - t_lo1, d + 1) for d in (-1, 0, 1) if 0 <= n + d < NT]
                        kv = kbw0_v
                        vv = vbw0_v
                        kres, vres = "kbw0", "vbw0"
                    else:
                        ds = [d for d in range(-dc, dc + 1) if 0 <= c + d < NC_]
                        kts = [(i, d + dc) for i, d in enumerate(ds)]
                        wi = m % 2
                        c_lo = c + ds[0]
                        nkt = len(ds)
                        kv = kbwm[g][wi].rearrange("p (q t) -> p q t", q=2)
                        vv = vbwm[g][wi].rearrange("p (n c) -> p n c", c=260)
                        kres, vres = ("kbwm", g, wi), ("vbwm", g, wi)
                        crg = [c + d for d in ds]
                        for q in range(2):
                            S.add("sp", lambda e, g=g, m=m, c_lo=c_lo, nkt=nkt, kv=kv, q=q: e.dma_start(
                                out=kv[:, q, 0:nkt * 128].rearrange("p (n t) -> p n t", n=nkt),
                                in_=KBT[si][g, q].rearrange("p (cc mm t) -> p cc mm t", mm=4, t=128)[:, c_lo:c_lo + nkt, m, :]),
                                reads=[("KBT", si, g, cc) for cc in crg], writes=[kres], dma=True)
                        S.add("sp", lambda e, g=g, m=m, c_lo=c_lo, nkt=nkt, vv=vv: e.dma_start(
                            out=vv[:, 0:nkt, :],
                            in_=VB[si][g].rearrange("(cc mm p) c -> p cc mm c", mm=4, p=128)[:, c_lo:c_lo + nkt, m, :]),
                            reads=[("VB", si, g, cc) for cc in crg], writes=[vres], dma=True)
                    nk = len(kts)
                    k_lo = kts[0][1]
                    ob = 4 + octr[0] % 2
                    octr[0] += 1
                    for h in range(4):
                        j, e2 = h // 2, h % 2
                        si_ = sctr[0] % 2
                        sctr[0] += 1
                        sb_ = 2 * si_
                        Sb = bank(sb_, 2)

                        def fqk(e, g=g, h=h, j=j, e2=e2, m=m, kts=kts, Sb=Sb, kv=kv, nk=nk, k_lo=k_lo):
                            n0 = min(nk, 4)
                            e.matmul(Sb[:, 0:n0 * 128], lhsT=ident,
                                     rhs=tabB[g].rearrange("p (h x) -> p h x", h=4)[:, h, k_lo * 128:(k_lo + n0) * 128],
                                     start=True, stop=False)
                            if nk > 4:
                                e.matmul(Sb[:, 512:640], lhsT=ident,
                                         rhs=tabB[g].rearrange("p (h x) -> p h x", h=4)[:, h, (k_lo + 4) * 128:(k_lo + 5) * 128],
                                         start=True, stop=False)
                            for i, (ws, _) in enumerate(kts):
                                ins = e.matmul(Sb[:, i * 128:i * 128 + 128],
                                               lhsT=kv[64 * e2:64 * e2 + 64, j, ws * 128:ws * 128 + 128],
                                               rhs=qbT_v[g][64 * e2:64 * e2 + 64, j, 128 * m:128 * m + 128],
                                               start=False, stop=(i == min(nk, 4) - 1 or i == nk - 1))
                            return ins
                        S.add("pe", fqk, reads=[kres, ("qbT", g), f"tabB{g}", "ident"], writes=[PB(sb_), PB(sb_ + 1)])
                        S.add("act", lambda e, si_=si_, nk=nk, Sb=Sb: e.activation(out=ptb[si_][:, 0:nk * 128], in_=Sb[:, 0:nk * 128], func=AF.Exp),
                              reads=[PB(sb_), PB(sb_ + 1)], writes=[("ptb", si_)])

                        def fpv(e, h=h, kts=kts, si_=si_, ob=ob, nk=nk, vv=vv):
                            for i, (ws, _) in enumerate(kts):
                                ins = e.matmul(bank(ob)[0:65, 128 * h:128 * h + 128],
                                               lhsT=vv[:, ws, h * 65:h * 65 + 65],
                                               rhs=ptb[si_][:, i * 128:i * 128 + 128], start=(i == 0), stop=(i == nk - 1))
                            return ins
                        S.add("pe", fpv, reads=[("ptb", si_), vres], writes=[PB(ob)])
                    O = bank(ob)[0:65, :].rearrange("p (h q) -> p h q", h=4)
                    tres = [("tot", m)] if g == 0 else [("tot", mm) for mm in range(4)]
                    if g == 0:
                        S.add("act", lambda e, O=O, m=m: e.activation(out=tot_v[0:65, :, 128 * m:128 * m + 128], in_=O, func=AF.Copy),
                              reads=[PB(ob)], writes=tres)
                    elif g == 1:
                        tv = tot_v[0:65, :, :].rearrange("p h (l mm) -> p h mm l", mm=4)[:, :, m, :]
                        S.add("dve", lambda e, O=O, tv=tv: e.tensor_tensor(out=tv, in0=O, in1=tv, op=ALU.add),
                              reads=[PB(ob)] + tres, writes=tres)
                    else:
                        tv = tot_v[0:65, :, :].rearrange("p h (l mm r) -> p h mm r l", mm=4, r=4)[:, :, m]
                        S.add("dve", lambda e, O=O, tv=tv: e.tensor_tensor(
                            out=tv, in0=O.rearrange("p h (r l) -> p h r l", r=4), in1=tv, op=ALU.add),
                            reads=[PB(ob)] + tres, writes=tres)
            for h in range(4):
                normalize(tot_v[0:65, h, :], [("tot", mm) for mm in range(4)], obst_v[64 * (h % 2):64 * (h % 2) + 64, h // 2, :], "obst")
            S.add("pool", lambda e, t0=t0: e.dma_start(out=OBT[si][:, :, t0:t0 + CH].rearrange("q p t -> p q t"),
                                                       in_=obst_v),
                  reads=["obst"], writes=[("OBT", si, c)], dma=True)

    def pass3(si):
        T = T_list[si]
        NC_ = T // CH
        S.filler_bank = None
        A.reset(PERM_MARK)
        NWB_ = 6
        wbuf = [A.alloc(4096, BF16) for _ in range(NWB_)]
        x1 = [A.alloc(4 * D, F32) for _ in range(2)]
        oat = A.alloc(4 * CH, BF16)
        oat_v = oat.rearrange("p (q t) -> p q t", q=4)
        obt = A.alloc(2 * CH, BF16)
        obt_v = obt.rearrange("p (q t) -> p q t", q=2)
        sga = A.alloc(CH, F32)
        sgb = A.alloc(CH, F32)
        tm1 = A.alloc(CH, F32)
        tm2 = A.alloc(CH, F32)
        mT = A.alloc(8 * CH, BF16)
        mT_v = mT.rearrange("p (k t) -> p k t", k=8)
        hmb = [A.alloc(D, BF16) for _ in range(2)]
        hmT = A.alloc(8 * CH, BF16)
        hmT_v = hmT.rearrange("p (k t) -> p k t", k=8)
        sq2 = [A.alloc(256, F32) for _ in range(2)]
        u2 = [A.alloc(256, BF16) for _ in range(3)]
        mst = A.alloc(16, F32)
        gmlp = A.alloc(D, F32)
        S.add("sp", lambda e: e.dma_start(out=gmlp, in_=g_mlp[0].partition_broadcast(128)), writes=["gmlp"], dma=True)
        wctr = [0]
        WINv = WIN.rearrange("(k p) n -> p k n", p=128)
        WOv = WO.rearrange("(k p) n -> p k n", p=128)
        WUPv = WUP.rearrange("(k p) n -> p k n", p=128)
        WAv = WA.rearrange("(q p) n -> p q n", p=128)
        WBv = WB.rearrange("(q p) n -> p q n", p=128)

        def wload(src_ap, wname, **kw):
            i = wctr[0] % NWB_
            wctr[0] += 1
            n = 1
            for s in src_ap.shape[1:]:
                n *= s
            P = src_ap.shape[0]
            dst = wbuf[i][0:P, 0:n]
            if len(src_ap.shape) == 3:
                dst = dst.rearrange("p (a b) -> p a b", a=src_ap.shape[1])
            S.add("sp", lambda e, dst=dst, src_ap=src_ap: e.dma_start(out=dst, in_=src_ap), reads=WRES[wname], writes=[("wbuf", i)], dma=True)
            return dst, ("wbuf", i)

        c_rng = range(2, 6) if si in half else range(NC_)
        for c in c_rng:
            par = c % 2
            X1 = x1[par]
            X1v = X1.rearrange("p (t d) -> p t d", t=4)
            t0 = c * CH
            hTc, hres = Hload(si, c)
            S.add("sp", lambda e, X1v=X1v, t0=t0: e.dma_start(out=X1v, in_=xs[si][t0:t0 + CH, :].rearrange("(t p) d -> p t d", p=128)),
                  writes=[("x1", par)], dma=True)
            S.add("sp", lambda e, t0=t0: e.dma_start(out=oat_v, in_=OAT[si][:, :, t0:t0 + CH].rearrange("q p t -> p q t")),
                  reads=[("OAT", si, c)], writes=["oat"], dma=True)
            S.add("sp", lambda e, t0=t0: e.dma_start(out=obt_v, in_=OBT[si][:, :, t0:t0 + CH].rearrange("q p t -> p q t")),
                  reads=[("OBT", si, c)], writes=["obt"], dma=True)
            for jg in range(2):
                wga, rga = wload(WINv[:, :, 3840 + 512 * jg:3840 + 512 * jg + 512], "WIN")
                wgb, rgb = wload(WINv[:, :, 4864 + 512 * jg:4864 + 512 * jg + 512], "WIN")
                wa, ra = wload(WAv[:, :, 512 * jg:512 * jg + 512], "WA")
                wb, rb = wload(WBv[:, :, 512 * jg:512 * jg + 512], "WB")
                for jj in range(4):
                    j = 4 * jg + jj
                    b0 = 4 * (j % 2)
                    cs = slice(128 * jj, 128 * jj + 128)

                    def fg(e, w, b, cs=cs, hTc=hTc):
                        for k in range(8):
                            ins = e.matmul(bank(b), lhsT=w[:, k, cs], rhs=hTc[:, k, :], start=(k == 0), stop=(k == 7))
                        return ins
                    S.add("pe", lambda e, w=wga, b=b0, fg=fg: fg(e, w, b), reads=hres + [rga], writes=[PB(b0)])
                    S.add("pe", lambda e, w=wgb, b=b0 + 1, fg=fg: fg(e, w, b), reads=hres + [rgb], writes=[PB(b0 + 1)])

                    def fa(e, wa=wa, b=b0 + 2, cs=cs):
                        for h in range(4):
                            ins = e.matmul(bank(b), lhsT=wa[:, h, cs], rhs=oat_v[:, h, :], start=(h == 0), stop=(h == 3))
                        return ins
                    S.add("pe", fa, reads=["oat", ra], writes=[PB(b0 + 2)])

                    def fb(e, wb=wb, b=b0 + 3, cs=cs):
                        for h in range(2):
                            ins = e.matmul(bank(b), lhsT=wb[:, h, cs], rhs=obt_v[:, h, :], start=(h == 0), stop=(h == 1))
                        return ins
                    S.add("pe", fb, reads=["obt", rb], writes=[PB(b0 + 3)])
                    S.add("act", lambda e, b=b0: e.activation(out=sga, in_=bank(b), func=AF.Sigmoid), reads=[PB(b0)], writes=["sga"])
                    S.add("act", lambda e, b=b0 + 1: e.activation(out=sgb, in_=bank(b), func=AF.Sigmoid), reads=[PB(b0 + 1)], writes=["sgb"])
                    S.add("dve", lambda e, b=b0 + 2: e.tensor_tensor(out=tm1, in0=sga, in1=bank(b), op=ALU.mult),
                          reads=["sga", PB(b0 + 2)], writes=["t1"])
                    S.add("dve", lambda e, b=b0 + 3: e.tensor_tensor(out=tm2, in0=sgb, in1=bank(b), op=ALU.mult),
                          reads=["sgb", PB(b0 + 3)], writes=["t2"])
                    S.add("pool", lambda e, j=j: e.tensor_tensor(out=mT_v[:, j, :], in0=tm1, in1=tm2, op=ALU.add),
                          reads=["t1", "t2"], writes=[("mT", j)])
            mres = [("mT", j) for j in range(8)]
            for hf in range(2):
                wo, ro = wload(WOv[:, :, 512 * hf:512 * hf + 512], "WO")
                for t in range(4):
                    b = (t % 2)

                    def fo(e, wo=wo, t=t, b=b):
                        for k in range(8):
                            ins = e.matmul(bank(b), lhsT=mT_v[:, k, 128 * t:128 * t + 128], rhs=wo[:, k, :], start=(k == 0), stop=(k == 7))
                        return ins
                    S.add("pe", fo, reads=mres + [ro], writes=[PB(b)])
                    xs_ = X1v[:, t, 512 * hf:512 * hf + 512]
                    S.add("dve", lambda e, xs_=xs_, b=b: e.tensor_tensor(out=xs_, in0=xs_, in1=bank(b), op=ALU.add),
                          reads=[("x1", par), PB(b)], writes=[("x1", par)])
            for t in range(4):
                s2 = t % 2
                ss = mst[:, t:t + 1]
                sd = mst[:, 4 + t:5 + t]
                rs = mst[:, 8 + t:9 + t]
                xv = X1v[:, t, :]
                S.add("act", lambda e, xv=xv, ss=ss: e.activation(out=junk, in_=xv, func=AF.Square, accum_out=ss),
                      reads=[("x1", par)], writes=["junk", ("mss", t)])
                S.add("act", lambda e, ss=ss, sd=sd: e.activation(out=sd, in_=ss, func=AF.Sqrt, scale=1.0 / D, bias=epst[:, 0:1]),
                      reads=[("mss", t), "eps"], writes=[("msd", t)])
                S.add("dve", lambda e, sd=sd, rs=rs: e.reciprocal(out=rs, in_=sd), reads=[("msd", t)], writes=[("mrs", t)])
                S.add("dve", lambda e, xv=xv, rs=rs, s2=s2: e.scalar_tensor_tensor(
                    out=hmb[s2], in0=xv, scalar=rs, in1=gmlp, op0=ALU.mult, op1=ALU.mult),
                    reads=[("x1", par), ("mrs", t), "gmlp"], writes=[("hmb", s2)])
                tb = 6 + s2
                pst = bankb(tb)

                def f(e, s2=s2, pst=pst):
                    for k in range(8):
                        ins = e.transpose(out=pst[:, 128 * k:128 * k + 128], in_=hmb[s2][:, 128 * k:128 * k + 128], identity=ident)
                    return ins
                S.add("pe", f, reads=[("hmb", s2), "ident"], writes=[PB(tb)])
                S.add("act", lambda e, pst=pst, t=t: e.activation(
                    out=hmT_v[:, :, 128 * t:128 * t + 128], in_=pst.rearrange("p (k t) -> p k t", k=8), func=AF.Copy),
                    reads=[PB(tb)], writes=[("hmT", t)])
            uctr = [0]
            for hh in range(2):
                pend = None
                hm_res = [("hmT", 2 * hh), ("hmT", 2 * hh + 1)]
                for fgp in range(8):
                    wup, rup = wload(WUPv[:, :, 512 * fgp:512 * fgp + 512], "WUP")
                    wdn, rdn = wload(WDN[512 * fgp:512 * fgp + 512, :].rearrange("(k p) n -> p k n", p=128), "WDN")
                    for f4 in range(4):
                        fb_ = 4 * fgp + f4
                        ub = 4 + fb_ % 2
                        ui = uctr[0] % 3
                        si2 = uctr[0] % 2
                        uctr[0] += 1

                        def fup(e, wup=wup, f4=f4, ub=ub, hh=hh):
                            for k in range(8):
                                ins = e.matmul(bank(ub)[:, 0:256], lhsT=wup[:, k, 128 * f4:128 * f4 + 128],
                                               rhs=hmT_v[:, k, 256 * hh:256 * hh + 256], start=(k == 0), stop=(k == 7))
                            return ins
                        S.add("pe", fup, reads=hm_res + [rup], writes=[PB(ub)])
                        S.add("act", lambda e, ub=ub, si2=si2: e.activation(out=sq2[si2], in_=bank(ub)[:, 0:256], func=AF.Square),
                              reads=[PB(ub)], writes=[("sq2", si2)])
                        S.add("dve", lambda e, ub=ub, si2=si2, ui=ui: e.scalar_tensor_tensor(
                            out=u2[ui], in0=bank(ub)[:, 0:256], scalar=0.0, in1=sq2[si2], op0=ALU.is_gt, op1=ALU.mult),
                            reads=[PB(ub), ("sq2", si2)], writes=[("u2", ui)])

                        def fdn(e, wdn=wdn, f4=f4, ui=ui, fb_=fb_):
                            for t2 in range(2):
                                for h2 in range(2):
                                    ins = e.matmul(bank(2 * t2 + h2), lhsT=u2[ui][:, 128 * t2:128 * t2 + 128],
                                                   rhs=wdn[:, f4, 512 * h2:512 * h2 + 512], start=(fb_ == 0), stop=(fb_ == 31))
                            return ins
                        if pend is not None:
                            S.add("pe", pend[0], reads=pend[1], writes=[PB(0), PB(1), PB(2), PB(3)])
                        pend = (fdn, [("u2", ui), rdn])
                S.add("pe", pend[0], reads=pend[1], writes=[PB(0), PB(1), PB(2), PB(3)])
                for t2 in range(2):
                    for h2 in range(2):
                        xs_ = X1v[:, 2 * hh + t2, 512 * h2:512 * h2 + 512]
                        b = 2 * t2 + h2
                        S.add("dve", lambda e, xs_=xs_, b=b: e.tensor_tensor(out=xs_, in0=xs_, in1=bank(b), op=ALU.add),
                              reads=[("x1", par), PB(b)], writes=[("x1", par)])
            y0 = (c - c_rng[0]) * CH
            S.add("pool", lambda e, X1v=X1v, y0=y0: e.dma_start(out=ys[si][y0:y0 + CH, :].rearrange("(t p) d -> p t d", p=128), in_=X1v),
                  reads=[("x1", par)], writes=[("y", si, c)], dma=True)

    step = 0
    for si in range(NS):
        for pf in (pass1, pass2, pass3):
            step += 1
            if step <= upto:
                pf(si)
                S.barrier()

    if max_ops is not None:
        S.ops = S.ops[:max_ops]
    print('n_ops', len(S.ops))
    S.emit(nc, stack)
    stack.close()
    return nc, A.peak


def _special_tables(rpb, hh, R=64):
    out = np.full((7, 4, 128, 2, 6, 64), NEGM, np.float32)
    kc = np.arange(64)[:, None]
    qc = np.arange(64)[None, :]
    cs = np.clip(qc - 8, 0, 48)
    cval = (kc >= cs) & (kc < cs + 16)
    cidx = np.clip(kc - qc + 15, 0, 30)
    for i, lr in enumerate((16, 17, 18, 19, 45, 46, 47)):
        gr = lr - 16 + 32 * hh
        rs = min(max(gr - 4, 0), R - 8)
        wrow = 12 if i < 4 else 40
        for b6 in range(6):
            for kr2 in range(2):
                kgr = wrow + 2 * b6 + kr2 - 16 + 32 * hh
                if not (rs <= kgr < rs + 8):
                    continue
                vals = rpb[:, kgr - gr + 7, :][:, cidx]
                vals = np.where(cval[None], vals, NEGM)
                for j in range(4):
                    for e2 in range(2):
                        out[i, j, kr2 * 64:(kr2 + 1) * 64, e2, b6, :] = vals[2 * j + e2]
    return out.reshape(28, 128, 768)


_CACHE = {}


def _common_inputs(norm_mix, w_in, q_norm_a, k_norm_a, q_norm_b, k_norm_b, rpb_a, t5_bias,
                   w_branch_a, w_branch_b, w_out, norm_mlp, w_up, w_down):
    f = lambda a: np.ascontiguousarray(np.asarray(a, dtype=np.float32))
    ga, ma, gb, mb = _tables(np.asarray(rpb_a, np.float32)[0], np.asarray(t5_bias, np.float32))
    d = {
        "w_in": f(w_in[0]), "w_a": f(w_branch_a[0]), "w_b": f(w_branch_b[0]), "w_o": f(w_out[0]),
        "w_up": f(w_up[0]), "w_dn": f(w_down[0]),
        "g_mix": f(norm_mix[0]).reshape(1, D), "g_mlp": f(norm_mlp[0]).reshape(1, D),
        "g_qa": f(q_norm_a[0]).reshape(1, 64), "g_ka": f(k_norm_a[0]).reshape(1, 64),
        "g_qb": f(q_norm_b[0]).reshape(1, 64), "g_kb": f(k_norm_b[0]).reshape(1, 64),
        "tA_g": ga.reshape(128, -1), "tA_m": ma.reshape(128, -1),
    }
    for g in range(3):
        d[f"tB_g{g}"] = gb[g].reshape(128, -1)
        d[f"tB_m{g}"] = mb[g].reshape(128, -1)
    return d


def kernel(x_prompt, x_sample, norm_mix, w_in, q_norm_a, k_norm_a, q_norm_b, k_norm_b, rpb_a,
           t5_bias, w_branch_a, w_branch_b, w_out, norm_mlp, w_up, w_down):
    x_prompt = np.asarray(x_prompt, np.float32)
    x_sample = np.asarray(x_sample, np.float32)
    Bp, Tp, _ = x_prompt.shape
    Bs, Ts, _ = x_sample.shape
    key = (Ts, Tp)
    assert Tp == 8 * CH and Bp * 2 == 8
    if key not in _CACHE:
        _CACHE[key] = build_program([Ts, Tp], half={1})[0]
    nc = _CACHE[key]
    common = _common_inputs(norm_mix, w_in, q_norm_a, k_norm_a, q_norm_b, k_norm_b, rpb_a, t5_bias,
                            w_branch_a, w_branch_b, w_out, norm_mlp, w_up, w_down)
    rpb = np.asarray(rpb_a, np.float32)[0]
    n = 8
    in_maps = []
    for c in range(n):
        m = dict(common)
        m["x0"] = np.ascontiguousarray(x_sample[c % Bs])
        p, hh = c // 2, c % 2
        xl = np.zeros((8 * CH, D), np.float32)
        s1 = np.zeros((128, 8), np.float32)
        if hh == 0:
            xl[2 * CH:] = x_prompt[p, 0:6 * CH]
            s1[:, 2:] = 1.0
        else:
            xl[:6 * CH] = x_prompt[p, 2 * CH:]
            s1[:, :6] = 1.0
        m["x1"] = xl
        m["slot1"] = s1
        m["tS"] = _special_tables(rpb, hh)
        in_maps.append(m)
    res = run_bass_kernel_spmd(nc, in_maps, core_ids=list(range(n)))
    y_s = np.stack([res.results[c]["y0"] for c in range(Bs)], axis=0).astype(np.float32)
    y_p = np.zeros((Bp, Tp, D), np.float32)
    for c in range(n):
        p, hh = c // 2, c % 2
        y_p[p, hh * 4 * CH:(hh + 1) * 4 * CH] = res.results[c]["y1"]
    return (y_p, y_s)
```
